# Optimizing a Trainium2 kernel written in Bass

```python
import math
import jax, jax.numpy as jnp
from jax import lax
import numpy as np

D_MODEL = 1024
BATCH = 4
SEQ = 8192
DEPTH = 1

N_META = 16
BLOCK = 128
N_PAD = BLOCK - N_META
D_MIX = D_MODEL
DIFF_WIDTH = D_MIX // 2
DIFF_V_DIM = 128
DIFF_QK_DIM = DIFF_V_DIM // 2
DIFF_HEADS = DIFF_WIDTH // DIFF_V_DIM
FOX_WIDTH = D_MIX - DIFF_WIDTH
FOX_HEAD_DIM = 64
FOX_HEADS = FOX_WIDTH // FOX_HEAD_DIM
ROPE_DIMS = DIFF_QK_DIM // 4
ROPE_THETA = 500000.0
D_FF = ((8 * D_MODEL // 3 + 255) // 256) * 256
RMS_EPS = 1e-6
SUBLN_EPS = 1e-5
NEG_INF = -1e30
DIFF_Q_COLS = DIFF_HEADS * 2 * DIFF_QK_DIM
DIFF_K_COLS = DIFF_HEADS * 2 * DIFF_QK_DIM
DIFF_V_COLS = DIFF_HEADS * DIFF_V_DIM
FOX_Q_COLS = FOX_HEADS * FOX_HEAD_DIM
FOX_K_COLS = FOX_HEADS * FOX_HEAD_DIM
FOX_V_COLS = FOX_HEADS * FOX_HEAD_DIM
FOX_F_COLS = FOX_HEADS
IN_COLS = DIFF_Q_COLS + DIFF_K_COLS + DIFF_V_COLS + FOX_Q_COLS + FOX_K_COLS + FOX_V_COLS + FOX_F_COLS
SPLITS = list(np.cumsum([DIFF_Q_COLS, DIFF_K_COLS, DIFF_V_COLS, FOX_Q_COLS, FOX_K_COLS, FOX_V_COLS]))

kernel_name = "hybrid_diffattn_fox_macaron_meta"


def lambda_init_fn(layer_idx):
    return 0.8 - 0.6 * math.exp(-0.3 * layer_idx)


def rmsnorm(x, g, eps=RMS_EPS):
    xf = x.astype(jnp.float32)
    y = xf * lax.rsqrt(jnp.mean(xf * xf, axis=-1, keepdims=True) + eps)
    return (y * g.astype(jnp.float32)).astype(x.dtype)


def swiglu(x, w_gate_up, w_down):
    g, u = jnp.split(x @ w_gate_up, 2, axis=-1)
    return (jax.nn.silu(g) * u) @ w_down


def apply_partial_rope(t, cos, sin):
    tf = t.astype(jnp.float32)
    half = ROPE_DIMS // 2
    x1 = tf[..., :half]
    x2 = tf[..., half:ROPE_DIMS]
    rot = jnp.concatenate([x1 * cos - x2 * sin, x2 * cos + x1 * sin, tf[..., ROPE_DIMS:]], axis=-1)
    return rot.astype(t.dtype)


def hybrid_mixer(hn, w_in, b_forget, lam_q1, lam_k1, lam_q2, lam_k2, subln_g, w_out, lambda_init):
    B, L, _ = hn.shape
    Lp = L + N_PAD
    n_blocks = Lp // BLOCK
    proj = hn @ w_in
    dq, dk, dv, fq, fk, fv, fl = jnp.split(proj, SPLITS, axis=-1)
    pad = lambda t: jnp.pad(t, ((0, 0), (N_PAD, 0), (0, 0)))

    pos = (jnp.arange(Lp, dtype=jnp.int32) - N_PAD).astype(jnp.float32)
    inv_freq = jnp.power(ROPE_THETA, -jnp.arange(0, ROPE_DIMS, 2, dtype=jnp.float32) / ROPE_DIMS)
    ang = pos[:, None] * inv_freq[None, :]
    cos = jnp.cos(ang)[None, :, None, None, :]
    sin = jnp.sin(ang)[None, :, None, None, :]

    dq = apply_partial_rope(pad(dq).reshape(B, Lp, DIFF_HEADS, 2, DIFF_QK_DIM), cos, sin).transpose(0, 2, 3, 1, 4)
    dk = apply_partial_rope(pad(dk).reshape(B, Lp, DIFF_HEADS, 2, DIFF_QK_DIM), cos, sin).transpose(0, 2, 3, 1, 4)
    dv = pad(dv).reshape(B, Lp, DIFF_HEADS, DIFF_V_DIM).transpose(0, 2, 1, 3)
    lam = (jnp.exp(jnp.sum(lam_q1.astype(jnp.float32) * lam_k1.astype(jnp.float32)))
           - jnp.exp(jnp.sum(lam_q2.astype(jnp.float32) * lam_k2.astype(jnp.float32)))
           + lambda_init)

    fq = pad(fq).reshape(B, Lp, FOX_HEADS, FOX_HEAD_DIM).transpose(0, 2, 1, 3)
    fk = pad(fk).reshape(B, Lp, FOX_HEADS, FOX_HEAD_DIM).transpose(0, 2, 1, 3)
    fv = pad(fv).reshape(B, Lp, FOX_HEADS, FOX_HEAD_DIM).transpose(0, 2, 1, 3)
    log_f = jax.nn.log_sigmoid(fl.astype(jnp.float32) + b_forget.astype(jnp.float32))
    log_f = jnp.pad(log_f, ((0, 0), (N_PAD, 0), (0, 0)))
    cum = jnp.cumsum(log_f, axis=1).transpose(0, 2, 1)

    diff_scale = DIFF_QK_DIM ** -0.5
    fox_scale = FOX_HEAD_DIM ** -0.5
    kidx = jnp.arange(Lp, dtype=jnp.int32)

    def block(i):
        start = i * BLOCK
        qidx = start + jnp.arange(BLOCK, dtype=jnp.int32)
        valid = (kidx[None, :] <= qidx[:, None]) & (kidx[None, :] >= N_PAD)
        q_d = lax.dynamic_slice_in_dim(dq, start, BLOCK, axis=3)
        s_d = jnp.einsum('bhcqd,bhckd->bhcqk', q_d, dk, preferred_element_type=jnp.float32) * diff_scale
        p_d = jax.nn.softmax(jnp.where(valid, s_d, NEG_INF), axis=-1)
        a_d = p_d[:, :, 0] - lam * p_d[:, :, 1]
        o_d = jnp.einsum('bhqk,bhkd->bhqd', a_d.astype(dv.dtype), dv)
        o_d = rmsnorm(o_d, subln_g, SUBLN_EPS) * (1.0 - lambda_init)
        q_f = lax.dynamic_slice_in_dim(fq, start, BLOCK, axis=2)
        c_q = lax.dynamic_slice_in_dim(cum, start, BLOCK, axis=2)
        s_f = (jnp.einsum('bhqd,bhkd->bhqk', q_f, fk, preferred_element_type=jnp.float32) * fox_scale
               + c_q[..., :, None] - cum[..., None, :])
        p_f = jax.nn.softmax(jnp.where(valid, s_f, NEG_INF), axis=-1)
        o_f = jnp.einsum('bhqk,bhkd->bhqd', p_f.astype(fv.dtype), fv)
        return jnp.concatenate([
            o_d.transpose(0, 2, 1, 3).reshape(B, BLOCK, DIFF_WIDTH),
            o_f.transpose(0, 2, 1, 3).reshape(B, BLOCK, FOX_WIDTH)], axis=-1)

    outs = lax.map(block, jnp.arange(n_blocks, dtype=jnp.int32))
    o = outs.transpose(1, 0, 2, 3).reshape(B, Lp, D_MIX)[:, N_PAD:]
    return o @ w_out


def setup_inputs(seed: int = 0) -> dict:
    key = jax.random.key(seed)
    ks = jax.random.split(key, 20)
    f32 = jnp.float32
    nrm = lambda k, shape, scale: jax.random.normal(k, shape, f32) * scale
    gain = lambda k, shape: 1.0 + 0.02 * jax.random.normal(k, shape, f32)
    return {
        "x": jax.random.normal(ks[0], (BATCH, SEQ, D_MODEL), f32),
        "meta_tokens": nrm(ks[1], (N_META, D_MODEL), 1.0),
        "ffn1_norm_g": gain(ks[2], (DEPTH, D_MODEL)),
        "ffn1_w_gate_up": nrm(ks[3], (DEPTH, D_MODEL, 2 * D_FF), D_MODEL ** -0.5),
        "ffn1_w_down": nrm(ks[4], (DEPTH, D_FF, D_MODEL), D_FF ** -0.5),
        "mix_norm_g": gain(ks[5], (DEPTH, D_MODEL)),
        "w_in": nrm(ks[6], (DEPTH, D_MODEL, IN_COLS), D_MODEL ** -0.5),
        "b_forget": 1.0 + 0.1 * jax.random.normal(ks[7], (DEPTH, FOX_HEADS), f32),
        "lam_q1": nrm(ks[8], (DEPTH, DIFF_QK_DIM), 0.1),
        "lam_k1": nrm(ks[9], (DEPTH, DIFF_QK_DIM), 0.1),
        "lam_q2": nrm(ks[10], (DEPTH, DIFF_QK_DIM), 0.1),
        "lam_k2": nrm(ks[11], (DEPTH, DIFF_QK_DIM), 0.1),
        "diff_subln_g": gain(ks[12], (DEPTH, DIFF_V_DIM)),
        "w_out": nrm(ks[13], (DEPTH, D_MIX, D_MODEL), D_MIX ** -0.5),
        "ffn2_norm_g": gain(ks[14], (DEPTH, D_MODEL)),
        "ffn2_w_gate_up": nrm(ks[15], (DEPTH, D_MODEL, 2 * D_FF), D_MODEL ** -0.5),
        "ffn2_w_down": nrm(ks[16], (DEPTH, D_FF, D_MODEL), D_FF ** -0.5),
        "final_norm_g": gain(ks[17], (D_MODEL,)),
    }


def reference(x, meta_tokens, ffn1_norm_g, ffn1_w_gate_up, ffn1_w_down, mix_norm_g, w_in, b_forget,
              lam_q1, lam_k1, lam_q2, lam_k2, diff_subln_g, w_out, ffn2_norm_g, ffn2_w_gate_up,
              ffn2_w_down, final_norm_g):
    B = x.shape[0]
    meta = jnp.broadcast_to(meta_tokens.astype(x.dtype)[None], (B, N_META, D_MODEL))
    h = jnp.concatenate([meta, x], axis=1)
    for layer in range(DEPTH):
        h = h + 0.5 * swiglu(rmsnorm(h, ffn1_norm_g[layer]), ffn1_w_gate_up[layer], ffn1_w_down[layer])
        h = h + hybrid_mixer(rmsnorm(h, mix_norm_g[layer]), w_in[layer], b_forget[layer],
                             lam_q1[layer], lam_k1[layer], lam_q2[layer], lam_k2[layer],
                             diff_subln_g[layer], w_out[layer], lambda_init_fn(layer))
        h = h + 0.5 * swiglu(rmsnorm(h, ffn2_norm_g[layer]), ffn2_w_gate_up[layer], ffn2_w_down[layer])
    return rmsnorm(h, final_norm_g)[:, N_META:]
```

```python
import math
import numpy as np
import ml_dtypes
import concourse.bass as bass
import concourse.mybir as mybir
from concourse.bass_utils import run_bass_kernel_spmd

F32 = mybir.dt.float32
BF16 = mybir.dt.bfloat16
AF = mybir.ActivationFunctionType
ALU = mybir.AluOpType

D = 1024
DFF = 2816
NF = DFF // 128
SEQ = 8192
NMETA = 16
NPAD = 112
LP = 8320
NBLK = 65
NSLOT = 8
INC = 3080
LAMBDA_INIT = 0.8 - 0.6 * math.exp(-0.3 * 0)
ENG = ["pe", "act", "dve", "pool", "sp"]


def MM(out, lhsT, rhs, start, stop, skip=False):
    if skip:
        return lambda e: e.matmul(out, lhsT=lhsT, rhs=rhs, start=start, stop=stop, skip_group_check=True)
    return lambda e: e.matmul(out, lhsT=lhsT, rhs=rhs, start=start, stop=stop)


def TR(out, in_, idt):
    return lambda e: e.transpose(out, in_, idt)


def ACTF(out, in_, func, bias=None, scale=None):
    kw = {}
    if bias is not None:
        kw["bias"] = bias
    if scale is not None:
        kw["scale"] = scale
    return lambda e: e.activation(out=out, in_=in_, func=func, **kw)


def AMUL(out, in_, m):
    return lambda e: e.mul(out, in_, m)


def TT(out, in0, in1, op):
    return lambda e: e.tensor_tensor(out=out, in0=in0, in1=in1, op=op)


def TS(out, in0, s1, op0):
    return lambda e: e.tensor_scalar(out=out, in0=in0, scalar1=s1, scalar2=None, op0=op0)


def STT(out, in0, scalar, in1, op0, op1, accum=None):
    if accum is not None:
        return lambda e: e.scalar_tensor_tensor(out=out, in0=in0, scalar=scalar, in1=in1, op0=op0, op1=op1, accum_out=accum)
    return lambda e: e.scalar_tensor_tensor(out=out, in0=in0, scalar=scalar, in1=in1, op0=op0, op1=op1)


def CP(out, in_):
    return lambda e: e.tensor_copy(out=out, in_=in_)


def MS(ap, val):
    return lambda e: e.memset(ap, val)


def RCP(out, in_):
    return lambda e: e.reciprocal(out=out, in_=in_)


class Sem:
    __slots__ = ("h", "count", "dma")

    def __init__(self, nc, name, dma=False):
        self.h = nc.alloc_semaphore(name)
        self.count = 0
        self.dma = dma


class T:
    __slots__ = ("w", "r", "dsem", "name")

    def __init__(self, name=""):
        self.w = None
        self.r = {}
        self.dsem = {}
        self.name = name


class Prog:
    def __init__(self, nc):
        self.nc = nc
        self.streams = {e: [] for e in ENG}
        self.esem = {e: Sem(nc, "e_" + e) for e in ENG}
        self.waited = {e: {} for e in ENG}
        self.dsems = []
        self.free_dsems = {"sp": [], "pool": []}
        self.ninst = 0

    def _deps(self, eng, reads, writes):
        deps = {}
        own = self.esem[eng]

        def add(sv, is_w):
            if sv is None:
                return
            s, v = sv
            if s is own and (eng == "pe" or not is_w):
                return
            if s.dma:
                v = s.count
            if deps.get(s, 0) < v:
                deps[s] = v

        for t in reads:
            add(t.w, True)
        for t in writes:
            add(t.w, True)
            for s, v in t.r.items():
                add((s, v), False)
        out = []
        w = self.waited[eng]
        for s, v in deps.items():
            if w.get(s, 0) >= v:
                continue
            w[s] = v
            out.append((s.h, v))
        return out

    def op(self, eng, fn, reads=(), writes=()):
        waits = self._deps(eng, reads, writes)
        sem = self.esem[eng]
        sem.count += 1
        val = sem.count
        h = sem.h

        def emit(e):
            for sh, v in waits:
                e.wait_ge(sh, v)
            fn(e).then_inc(h, 1)

        self.streams[eng].append(emit)
        self.ninst += 1
        for t in writes:
            t.w = (sem, val)
            t.r = {}
        for t in reads:
            t.r[sem] = val

    def get_dsem(self, t, q):
        if q not in t.dsem:
            if self.free_dsems[q]:
                t.dsem[q] = self.free_dsems[q].pop()
            else:
                s = Sem(self.nc, "d%s%d" % (q, len(self.dsems)), dma=True)
                self.dsems.append(s)
                t.dsem[q] = s
        return t.dsem[q]

    def dma(self, q, out_ap, in_ap, st, reads=(), writes=()):
        waits = self._deps(q, reads, writes)
        ds = self.get_dsem(st, q)
        ds.count += 16
        val = ds.count
        h = ds.h

        def emit(e):
            for sh, v in waits:
                e.wait_ge(sh, v)
            e.dma_start(out=out_ap, in_=in_ap).then_inc(h, 16)

        self.streams[q].append(emit)
        self.ninst += 1
        for t in writes:
            t.w = (ds, val)
            t.r = {}
        for t in reads:
            t.r[ds] = val

    def barrier(self, release=()):
        sems = [s for s in list(self.esem.values()) + self.dsems if s.count > 0]
        for e in ENG:
            waits = []
            w = self.waited[e]
            for s in sems:
                if s is self.esem[e]:
                    continue
                if w.get(s, 0) >= s.count:
                    continue
                w[s] = s.count
                waits.append((s.h, s.count))
            if waits:
                def emit(en, waits=waits):
                    for sh, v in waits:
                        en.wait_ge(sh, v)
                self.streams[e].append(emit)
        for t in release:
            for q, sm in t.dsem.items():
                self.free_dsems[q].append(sm)
            t.dsem = {}


class Arena:
    def __init__(self, nc, nbytes):
        self.n32 = nbytes // 4
        self.h = nc.alloc_sbuf_tensor("arena", [128, self.n32], F32)
        self.off = 0
        self.mark_ = 0
        self.tokens = []

    def tile(self, shape, dtype, name=""):
        free = 1
        for s in shape[1:]:
            free *= s
        nb = free * (4 if dtype == F32 else 2)
        nb = (nb + 31) // 32 * 32
        assert self.off + nb <= self.n32 * 4, ("arena overflow", name, self.off, nb, self.n32 * 4)
        v = self.h[:, self.off // 4:(self.off + nb) // 4]
        if dtype != F32:
            v = v.bitcast(dtype)
        v = v[:, 0:free]
        if len(shape) == 3:
            v = v.rearrange("p (a b) -> p a b", a=shape[1])
        elif len(shape) == 4:
            v = v.rearrange("p (a b c) -> p a b c", a=shape[1], b=shape[2])
        if shape[0] < 128:
            v = v[0:shape[0]]
        self.off += nb
        t = T(name)
        self.tokens.append(t)
        return v, t

    def mark(self):
        self.mark_ = self.off
        self.tokens = []

    def reset(self):
        self.off = self.mark_
        toks = self.tokens
        self.tokens = []
        return toks


def build_program():
    nc = bass.Bass("TRN2", target_bir_lowering=False)
    P = Prog(nc)

    def din(name, shape, dt=F32):
        return nc.dram_tensor(name, list(shape), dt, kind="ExternalInput").ap()

    def dscr(name, shape, dt):
        return nc.dram_tensor(name, list(shape), dt, kind="Internal").ap()

    x_arr = din("x_arr", [LP, D])
    w_gu1 = din("w_gu1", [D, 2 * DFF])
    w_d1 = din("w_d1", [DFF, D])
    w_gu2 = din("w_gu2", [D, 2 * DFF])
    w_d2 = din("w_d2", [DFF, D])
    w_in = din("w_in", [D, INC])
    w_sw = din("w_sw", [D, 1024])
    w_out = din("w_out", [D, D])
    gcols = din("gcols", [128, 24])
    gfin = din("gfin", [D])
    bfg_d = din("bfg", [8])
    lam_d = din("lam", [4, 64])
    subg_d = din("subg", [128])
    cosK = din("cosK", [128, LP])
    sinK = din("sinK", [128, LP])
    cosQ = din("cosQ", [128, 4096])
    sinQ = din("sinQ", [128, 4096])
    cbf = din("cbf", [128, 256], BF16)
    cf32 = din("cf32", [128, 384])
    flags = din("flags", [128, 4])
    out_d = nc.dram_tensor("out", [4096, D], F32, kind="ExternalOutput").ap()

    h1_scr = dscr("h1_scr", [4096, D], F32)
    hnT_scr = dscr("hnT_scr", [128, 8, LP], BF16)
    KTd_scr = dscr("KTd_scr", [4, 128, LP], BF16)
    KTf_scr = dscr("KTf_scr", [8, 65, LP], BF16)
    Vd_scr = dscr("Vd_scr", [LP, 4, 130], BF16)
    Vf_scr = dscr("Vf_scr", [LP, 8, 66], BF16)
    QTd_scr = dscr("QTd_scr", [4, 128, 4096], BF16)
    QTf_scr = dscr("QTf_scr", [8, 65, 4096], BF16)
    oT_scr = dscr("oT_scr", [8, 128, 4096], BF16)

    arena = Arena(nc, (nc.sbuf_bytes_remaining - 256) // 32 * 32)
    psF = nc.alloc_psum_tensor("psF", [128, 7, 512], F32)
    psB = nc.alloc_psum_tensor("psB", [128, 1024], BF16)
    bankT = [T("bank%d" % i) for i in range(7)]
    psBT = T("psB")
    psB3 = psB[:, :].rearrange("p (c t) -> p c t", c=8)

    cb, cb_t = arena.tile([128, 256], BF16, "cbf")
    ident = cb[:, 0:128]
    tri = cb[:, 128:256]
    cf, cf_t = arena.tile([128, 384], F32, "cf32")
    triF = cf[:, 0:128]
    onesF = cf[:, 128:256]
    identF = cf[:, 256:384]
    gc, gc_t = arena.tile([128, 24], F32, "gcols")
    flg, flg_t = arena.tile([128, 4], F32, "flags")
    bfg, bfg_t = arena.tile([128, 8], F32, "bfg")
    lpos, lpos_t = arena.tile([128, NBLK, 8], F32, "lpos")
    P.dma("sp", cb, cbf, cb_t, writes=[cb_t])
    P.dma("sp", cf, cf32, cf_t, writes=[cf_t])
    P.dma("sp", gc, gcols, gc_t, writes=[gc_t])
    P.dma("sp", flg, flags, flg_t, writes=[flg_t])
    P.dma("sp", bfg, bfg_d.partition_broadcast(128), bfg_t, writes=[bfg_t])
    base_mark = arena.off
    arena.tokens = []

    def end_phase():
        arena.off = base_mark
        toks = arena.tokens
        arena.tokens = []
        P.barrier(release=toks)

    groups = [(0, 1, None)]
    for j in range(16):
        groups.append((128 + 512 * j, 4, (j // 2) if j % 2 == 0 else None))

    rr = {"cast": 0, "bank": 0}

    def next_bank():
        b = rr["bank"] % 7
        rr["bank"] += 1
        return b

    def load_weights(wd, K, N, dst, scale_cols, stg, order=None):
        dstv, dst_t = dst
        nblk = (N + 1023) // 1024
        toks = [T("wblk%d" % i) for i in range(nblk)]
        for bi in (order if order is not None else range(nblk)):
            n0 = bi * 1024
            w = min(1024, N - n0)
            for c in range(K // 128):
                sb, sb_t = stg[rr["cast"] % len(stg)]
                P.dma("sp", sb[:, 0:w], wd[c * 128:(c + 1) * 128, n0:n0 + w], sb_t, writes=[sb_t])
                o = dstv[:, c, n0:n0 + w]
                i = sb[:, 0:w]
                if scale_cols is not None:
                    sc = scale_cols[:, c:c + 1]
                    if rr["cast"] % 2 == 0:
                        P.op("act", AMUL(o, i, sc), reads=[sb_t, gc_t], writes=[toks[bi]])
                    else:
                        P.op("dve", TS(o, i, sc, ALU.mult), reads=[sb_t, gc_t], writes=[toks[bi]])
                else:
                    if rr["cast"] % 2 == 0:
                        P.op("act", ACTF(o, i, AF.Copy), reads=[sb_t], writes=[toks[bi]])
                    else:
                        P.op("dve", CP(o, i), reads=[sb_t], writes=[toks[bi]])
                rr["cast"] += 1
        return toks

    def rstd_chain(ss_ap, ss_t, ln_ap, ln_t, rs_ap, rs_t, inv_d, eps):
        P.op("act", ACTF(ln_ap, ss_ap, AF.Ln, bias=eps, scale=inv_d), reads=[ss_t], writes=[ln_t])
        P.op("act", ACTF(rs_ap, ln_ap, AF.Exp, scale=-0.5), reads=[ln_t], writes=[rs_t])

    def sumsq(src_ap, src_t, junk_ap, junk_t, acc_ap_, acc_t):
        P.op("dve", MS(acc_ap_, 0.0), writes=[acc_t])
        P.op("dve", STT(junk_ap, src_ap, 1.0, src_ap, ALU.mult, ALU.mult, accum=acc_ap_),
             reads=[src_t, acc_t], writes=[junk_t, acc_t])

    def transpose8(src_ap, src_t, dst_ap, dst_t):
        for c in range(8):
            P.op("pe", TR(psB[:, c * 128:(c + 1) * 128], src_ap[:, c * 128:(c + 1) * 128], ident),
                 reads=[src_t, cb_t], writes=[psBT])
        P.op("dve", CP(dst_ap, psB3), reads=[psBT], writes=[dst_t])

    def ffn_phase(w_gu, w_d, gcol, grp_list, load_block, post_block, extra_alloc):
        Wgu = arena.tile([128, 8, 2 * DFF], BF16, "Wgu")
        Wd = arena.tile([128, NF, D], BF16, "Wd")
        xa = [arena.tile([128, D], F32, "xa%d" % i) for i in range(2)]
        xd = [arena.tile([128, D], F32, "xd%d" % i) for i in range(3)]
        xn = arena.tile([128, D], BF16, "xn")
        xnT = [arena.tile([128, 8, 512], BF16, "xnT%d" % i) for i in range(2)]
        hT = arena.tile([128, NF, 512], BF16, "hT")
        sg = arena.tile([128, 512], F32, "sg")
        ss = arena.tile([128, 4], F32, "ss")
        lnv = arena.tile([128, 4], F32, "lnv")
        rstd = arena.tile([128, 4], F32, "rstd")
        extra = extra_alloc()
        extra["junk"] = xn
        extra["hn"] = xn
        Wgu_tk = load_weights(w_gu, D, 2 * DFF, Wgu, gcol, xa + xd, order=[0, 2, 3, 1, 4, 5])
        Wd_tk = load_weights(w_d, DFF, D, Wd, None, xa + xd)
        Wguv, Wgu_t = Wgu
        Wdv, Wd_t = Wd
        hTv, hT_t = hT
        xnv, xn_t = xn
        sgv, sg_t = sg
        ssv, ss_t = ss
        lnvv, lnv_t = lnv
        rsv, rs_t = rstd
        ctr = {"n": 0, "d": 0}
        ngrp = len(grp_list)

        def stage_n(gi, r):
            g = grp_list[gi]
            k = ctr["n"]
            ctr["n"] += 1
            buf = xa[k % 2]
            bv, b_t = buf
            col = k % 4
            load_block(g, r, buf)
            sumsq(bv[:, :], b_t, xnv[:, :], xn_t, ssv[:, col:col + 1], ss_t)
            rstd_chain(ssv[:, col:col + 1], ss_t, lnvv[:, col:col + 1], lnv_t, rsv[:, col:col + 1], rs_t, 1.0 / D, 1e-6)
            P.op("act", AMUL(xnv[:, :], bv[:, :], rsv[:, col:col + 1]), reads=[b_t, rs_t], writes=[xn_t])
            tv_, t_t = xnT[gi % 2]
            transpose8(xnv, xn_t, tv_[:, :, r * 128:(r + 1) * 128], t_t)

        def stage_gu(gi, f):
            tok0, nb, own = grp_list[gi]
            Tn = nb * 128
            tv_, t_t = xnT[gi % 2]
            pg = (rr["bank"] % 2) * 2
            rr["bank"] += 1
            for half, bank in ((0, pg), (1, pg + 1)):
                col = half * DFF + f * 128
                for c in range(8):
                    P.op("pe", MM(psF[:, bank, 0:Tn], Wguv[:, c, col:col + 128], tv_[:, c, 0:Tn], c == 0, c == 7),
                         reads=[Wgu_tk[col // 1024], t_t], writes=[bankT[bank]])
            P.op("act", ACTF(sgv[:, 0:Tn], psF[:, pg, 0:Tn], AF.Silu), reads=[bankT[pg]], writes=[sg_t])
            P.op("dve", TT(hTv[:, f, 0:Tn], sgv[:, 0:Tn], psF[:, pg + 1, 0:Tn], ALU.mult),
                 reads=[sg_t, bankT[pg + 1]], writes=[hT_t])

        dbuf = {}

        def d_load(gi, r):
            k = ctr["d"]
            ctr["d"] += 1
            buf = xd[k % 3]
            load_block(grp_list[gi], r, buf)
            dbuf[(gi, r)] = buf

        def stage_d(gi, r):
            bv, b_t = dbuf[(gi, r)]
            for half in range(2):
                bank = 4 + (rr["bank"] % 2)
                rr["bank"] += 1
                for f in range(NF):
                    P.op("pe", MM(psF[:, bank, :], hTv[:, f, r * 128:(r + 1) * 128], Wdv[:, f, half * 512:(half + 1) * 512],
                                  f == 0, f == NF - 1), reads=[hT_t, Wd_tk[0]], writes=[bankT[bank]])
                o = bv[:, half * 512:(half + 1) * 512]
                P.op("dve", STT(o, psF[:, bank, :], 0.5, o, ALU.mult, ALU.add), reads=[bankT[bank], b_t], writes=[b_t])

        for r in range(grp_list[0][1]):
            stage_n(0, r)
        pending = None
        for gi, g in enumerate(grp_list):
            tok0, nb, own = g
            nb_next = grp_list[gi + 1][1] if gi + 1 < ngrp else 0
            n_at = {4: 0, 9: 1, 14: 2, 19: 3}
            for f in range(NF):
                stage_gu(gi, f)
                if f == 1 and pending is not None:
                    pending()
                    pending = None
                if f in n_at and n_at[f] < nb_next:
                    stage_n(gi + 1, n_at[f])
                if f == 16:
                    d_load(gi, 0)
                    if nb > 1:
                        d_load(gi, 1)
            for r in range(nb):
                stage_d(gi, r)
                if pending is not None:
                    pending()
                    pending = None
                if r + 2 < nb:
                    d_load(gi, r + 2)
                buf = dbuf[(gi, r)]
                pending = (lambda g=g, r=r, buf=buf: post_block(g, r, buf, extra))
        if pending is not None:
            pending()
        end_phase()

    def a1_extra():
        return dict(hnTs=arena.tile([128, 8, 128], BF16, "hnTs"),
                    ss2=arena.tile([128, 4], F32, "ss2"), ln2=arena.tile([128, 4], F32, "ln2"),
                    rs2=arena.tile([128, 4], F32, "rs2"), k=[0])

    def a1_load(g, r, buf):
        tok0, nb, own = g
        bv, b_t = buf
        P.dma("sp", bv[:, :], x_arr[tok0 + r * 128: tok0 + (r + 1) * 128, :], b_t, writes=[b_t])

    def a1_post(g, r, buf, ex):
        tok0, nb, own = g
        bv, b_t = buf
        if own is not None:
            P.dma("pool", h1_scr[own * 512 + r * 128: own * 512 + (r + 1) * 128, :], bv[:, :], b_t, reads=[b_t])
        k = ex["k"][0]
        ex["k"][0] += 1
        col = k % 4
        ssv, ss_t = ex["ss2"]
        lv, l_t = ex["ln2"]
        rv, r_t = ex["rs2"]
        hv, h_t = ex["hn"]
        sumsq(bv[:, :], b_t, hv[:, :], h_t, ssv[:, col:col + 1], ss_t)
        rstd_chain(ssv[:, col:col + 1], ss_t, lv[:, col:col + 1], l_t, rv[:, col:col + 1], r_t, 1.0 / D, 1e-6)
        P.op("act", AMUL(hv[:, :], bv[:, :], rv[:, col:col + 1]), reads=[b_t, r_t], writes=[h_t])
        sv, s_t = ex["hnTs"]
        transpose8(hv, h_t, sv[:, :, :], s_t)
        P.dma("pool", hnT_scr[:, :, tok0 + r * 128: tok0 + (r + 1) * 128], sv[:, :, :], s_t, reads=[s_t])

    ffn_phase(w_gu1, w_d1, gc[:, 0:8], groups, a1_load, a1_post, a1_extra)

    Win = arena.tile([128, 8, INC], BF16, "Win")
    Wsw = arena.tile([128, 8, 1024], BF16, "Wsw")
    stg = [arena.tile([128, 1024], F32, "stg%d" % i) for i in range(3)]
    Win_tk = load_weights(w_in, D, INC, Win, gc[:, 8:16], stg, order=[0])
    Wsw_tk = load_weights(w_sw, D, 1024, Wsw, gc[:, 8:16], stg)
    Win_tk2 = load_weights(w_in, D, INC, Win, gc[:, 8:16], stg, order=[2, 1, 3])
    for _bi in (1, 2, 3):
        Win_tk[_bi] = Win_tk2[_bi]
    Winv, Win_t = Win
    Wswv, Wsw_t = Wsw
    hnTb = [arena.tile([128, 8, 512], BF16, "hnTb%d" % i) for i in range(2)]
    csK = [arena.tile([128, 2, 512], F32, "csK%d" % i) for i in range(2)]
    csQ = [arena.tile([128, 2, 512], F32, "csQ%d" % i) for i in range(2)]
    t1 = [arena.tile([128, 512], F32, "t1_%d" % i) for i in range(2)]
    t2 = [arena.tile([128, 512], F32, "t2_%d" % i) for i in range(2)]
    kst = [arena.tile([128, 512], BF16, "kst%d" % i) for i in range(4)]
    vdst = [arena.tile([128, 4, 130], BF16, "vdst%d" % i) for i in range(2)]
    vd0 = arena.tile([128, 4, 130], BF16, "vd0")
    vfst = [arena.tile([128, 8, 66], BF16, "vfst%d" % i) for i in range(2)]
    vf0 = arena.tile([128, 8, 66], BF16, "vf0")
    zt = [arena.tile([128, 8], F32, "zt%d" % i) for i in range(2)]
    et = [arena.tile([128, 8], F32, "et%d" % i) for i in range(2)]
    for (v, t) in vdst + vfst + [vd0, vf0]:
        P.op("dve", MS(v, 1.0), writes=[t])
    P.op("dve", MS(vd0[0][0:NPAD], 0.0), writes=[vd0[1]])
    P.op("dve", MS(vf0[0][0:NPAD], 0.0), writes=[vf0[1]])
    kk = {"k": 0, "v": 0}

    def rope_proj(col0, cs, cs_t, dst_scr, m, Tn, c0, hv, h_t):
        ba, bb = next_bank(), next_bank()
        for c in range(8):
            P.op("pe", MM(psF[:, ba, 0:Tn], Winv[:, c, col0 + m * 128: col0 + (m + 1) * 128], hv[:, c, 0:Tn], c == 0, c == 7),
                 reads=[Win_tk[(col0 + m * 128) // 1024], h_t], writes=[bankT[ba]])
        for c in range(8):
            P.op("pe", MM(psF[:, bb, 0:Tn], Wswv[:, c, col0 + m * 128: col0 + (m + 1) * 128], hv[:, c, 0:Tn], c == 0, c == 7),
                 reads=[Wsw_tk[0], h_t], writes=[bankT[bb]])
        k = kk["k"]
        kk["k"] += 1
        av, a_t = t1[k % 2]
        bv, b_t = t2[k % 2]
        sv, s_t = kst[k % 4]
        P.op("dve", TT(av[:, 0:Tn], psF[:, ba, 0:Tn], cs[:, 0, 0:Tn], ALU.mult), reads=[bankT[ba], cs_t], writes=[a_t])
        P.op("dve", TT(bv[:, 0:Tn], psF[:, bb, 0:Tn], cs[:, 1, 0:Tn], ALU.mult), reads=[bankT[bb], cs_t], writes=[b_t])
        P.op("pool", TT(sv[:, 0:Tn], av[:, 0:Tn], bv[:, 0:Tn], ALU.add), reads=[a_t, b_t], writes=[s_t])
        P.dma("pool", dst_scr[m][:, c0:c0 + Tn], sv[:, 0:Tn], s_t, reads=[s_t])

    def plain_proj(col0, dst_scr, hp, Tn, c0, hv, h_t, scale):
        ba = next_bank()
        for c in range(8):
            P.op("pe", MM(psF[:, ba, 0:Tn], Winv[:, c, col0 + hp * 128: col0 + (hp + 1) * 128], hv[:, c, 0:Tn], c == 0, c == 7),
                 reads=[Win_tk[(col0 + hp * 128) // 1024], h_t], writes=[bankT[ba]])
        k = kk["k"]
        kk["k"] += 1
        sv, s_t = kst[k % 4]
        P.op("act", AMUL(sv[:, 0:Tn], psF[:, ba, 0:Tn], scale), reads=[bankT[ba]], writes=[s_t])
        P.dma("pool", dst_scr[2 * hp][0:64, c0:c0 + Tn], sv[0:64, 0:Tn], s_t, reads=[s_t])
        P.dma("pool", dst_scr[2 * hp + 1][0:64, c0:c0 + Tn], sv[64:128, 0:Tn], s_t, reads=[s_t])

    def a2_loads(gi):
        tok0, nb, own = groups[gi]
        Tn = nb * 128
        hv, h_t = hnTb[gi % 2]
        P.dma("sp", hv[:, :, 0:Tn], hnT_scr[:, :, tok0:tok0 + Tn], h_t, writes=[h_t])
        cv, c_t = csK[gi % 2]
        P.dma("sp", cv[:, 0, 0:Tn], cosK[:, tok0:tok0 + Tn], c_t, writes=[c_t])
        P.dma("sp", cv[:, 1, 0:Tn], sinK[:, tok0:tok0 + Tn], c_t, writes=[c_t])
        if own is not None:
            qv, q_t = csQ[own % 2]
            P.dma("sp", qv[:, 0, :], cosQ[:, own * 512:(own + 1) * 512], q_t, writes=[q_t])
            P.dma("sp", qv[:, 1, :], sinQ[:, own * 512:(own + 1) * 512], q_t, writes=[q_t])

    def a2_vblock(tok0, r, hv, h_t):
        blk = (tok0 + r * 128) // 128
        k = kk["v"]
        kk["v"] += 1
        lhs = [hv[:, c, r * 128:(r + 1) * 128] for c in range(8)]
        ba = next_bank()
        for c in range(8):
            P.op("pe", MM(psF[:, ba, :], lhs[c], Winv[:, c, 1024:1536], c == 0, c == 7), reads=[Win_tk[1], h_t], writes=[bankT[ba]])
        sv, s_t = vd0 if blk == 0 else vdst[k % 2]
        P.op("act", ACTF(sv[:, :, 0:128], psF[:, ba, :].rearrange("p (h n) -> p h n", h=4), AF.Copy),
             reads=[bankT[ba]], writes=[s_t])
        P.dma("pool", Vd_scr[blk * 128:(blk + 1) * 128], sv[:, :, :], s_t, reads=[s_t])
        bb = next_bank()
        for c in range(8):
            P.op("pe", MM(psF[:, bb, :], lhs[c], Winv[:, c, 2560:3072], c == 0, c == 7), reads=[Win_tk[2], h_t], writes=[bankT[bb]])
        fv, f_t = vf0 if blk == 0 else vfst[k % 2]
        P.op("dve", CP(fv[:, :, 0:64], psF[:, bb, :].rearrange("p (h n) -> p h n", h=8)), reads=[bankT[bb]], writes=[f_t])
        P.dma("pool", Vf_scr[blk * 128:(blk + 1) * 128], fv[:, :, :], f_t, reads=[f_t])
        bc = next_bank()
        for c in range(8):
            P.op("pe", MM(psF[:, bc, 0:8], lhs[c], Winv[:, c, 3072:3080], c == 0, c == 7), reads=[Win_tk[3], h_t], writes=[bankT[bc]])
        zv, z_t = zt[k % 2]
        ev, e_t = et[k % 2]
        P.op("dve", TT(zv[:, :], psF[:, bc, 0:8], bfg[:, :], ALU.add), reads=[bankT[bc], bfg_t], writes=[z_t])
        P.op("act", ACTF(ev[:, :], zv[:, :], AF.Exp, scale=-1.0), reads=[z_t], writes=[e_t])
        P.op("act", ACTF(lpos[:, blk, :], ev[:, :], AF.Ln, bias=1.0), reads=[e_t], writes=[lpos_t])

    def a2_group(gi):
        tok0, nb, own = groups[gi]
        Tn = nb * 128
        if gi + 1 < len(groups):
            a2_loads(gi + 1)
        hv, h_t = hnTb[gi % 2]
        cv, c_t = csK[gi % 2]
        for m in range(4):
            rope_proj(512, cv, c_t, KTd_scr, m, Tn, tok0, hv, h_t)
        for hp in range(4):
            plain_proj(2048, KTf_scr, hp, Tn, tok0, hv, h_t, 1.0)
        if own is not None:
            qv, q_t = csQ[own % 2]
            for m in range(4):
                rope_proj(0, qv, q_t, QTd_scr, m, Tn, own * 512, hv, h_t)
            for hp in range(4):
                plain_proj(1536, QTf_scr, hp, Tn, own * 512, hv, h_t, 0.125)
        for r in range(nb):
            a2_vblock(tok0, r, hv, h_t)

    a2_loads(0)
    for gi in range(len(groups)):
        a2_group(gi)
    P.op("dve", MS(lpos[0:NPAD, 0, :], 0.0), writes=[lpos_t])
    end_phase()

    cum, cum_t = arena.tile([128, NBLK, 8], F32, "cum")
    totB, tot_t = arena.tile([128, NBLK, 8], F32, "totB")
    offB, off_t = arena.tile([128, NBLK, 8], F32, "offB")
    sbtot, sbt_t = arena.tile([128, 16, 8], F32, "sbtot")
    ptot, pt_t = arena.tile([128, 8, 8], F32, "ptot")
    Ppre, pp_t = arena.tile([128, 9, 8], F32, "Ppre")
    offsb, osb_t = arena.tile([128, 16, 8], F32, "offsb")
    biasS = [arena.tile([128, 8 * i + 9, 8], F32, "biasS%d" % i) for i in range(NSLOT)]
    neglam, nl_t = arena.tile([128, 1], F32, "neglam")
    subg, subg_t = arena.tile([128, 1], F32, "subg")
    b_mark = arena.off

    lflat = lpos.rearrange("p a b -> p (a b)")
    for (lhs, dstv, dst_t) in ((triF, cum, cum_t), (onesF, totB, tot_t)):
        b0, b1 = next_bank(), next_bank()
        P.op("pe", MM(psF[:, b0, 0:512], lhs, lflat[:, 0:512], True, True), reads=[cf_t, lpos_t], writes=[bankT[b0]])
        P.op("pe", MM(psF[:, b1, 0:8], lhs, lflat[:, 512:520], True, True), reads=[cf_t, lpos_t], writes=[bankT[b1]])
        dflat = dstv.rearrange("p a b -> p (a b)")
        P.op("dve", CP(dflat[:, 0:512], psF[:, b0, 0:512]), reads=[bankT[b0]], writes=[dst_t])
        P.op("dve", CP(dflat[:, 512:520], psF[:, b1, 0:8]), reads=[bankT[b1]], writes=[dst_t])
    tv = totB[:, 1:65, :].rearrange("p (j r) h -> p j r h", r=4)
    ov = offB[:, 1:65, :].rearrange("p (j r) h -> p j r h", r=4)
    P.op("dve", TT(sbtot[:, :, :], tv[:, :, 0, :], tv[:, :, 1, :], ALU.add), reads=[tot_t], writes=[sbt_t])
    P.op("dve", TT(sbtot[:, :, :], sbtot[:, :, :], tv[:, :, 2, :], ALU.add), reads=[tot_t, sbt_t], writes=[sbt_t])
    P.op("dve", TT(sbtot[:, :, :], sbtot[:, :, :], tv[:, :, 3, :], ALU.add), reads=[tot_t, sbt_t], writes=[sbt_t])
    sbv = sbtot.rearrange("p (i two) h -> p i two h", two=2)
    P.op("dve", TT(ptot[:, :, :], sbv[:, :, 0, :], sbv[:, :, 1, :], ALU.add), reads=[sbt_t], writes=[pt_t])
    P.op("dve", CP(Ppre[:, 0, :], totB[:, 0, :]), reads=[tot_t], writes=[pp_t])
    for i in range(8):
        P.op("dve", TT(Ppre[:, i + 1, :], Ppre[:, i, :], ptot[:, i, :], ALU.add), reads=[pp_t, pt_t], writes=[pp_t])
    osv = offsb.rearrange("p (i two) h -> p i two h", two=2)
    P.op("dve", STT(osv[:, :, 0, :], sbv[:, :, 1, :], flg[:, 1:2], Ppre[:, 0:8, :], ALU.mult, ALU.add),
         reads=[sbt_t, pp_t, flg_t], writes=[osb_t])
    P.op("dve", STT(osv[:, :, 1, :], sbv[:, :, 0, :], flg[:, 2:3], Ppre[:, 0:8, :], ALU.mult, ALU.add),
         reads=[sbt_t, pp_t, flg_t, osb_t], writes=[osb_t])
    P.op("dve", MS(offB[:, 0, :], 0.0), writes=[off_t])
    P.op("dve", CP(ov[:, :, 0, :], offsb[:, :, :]), reads=[osb_t, off_t], writes=[off_t])
    for r in range(1, 4):
        P.op("dve", TT(ov[:, :, r, :], ov[:, :, r - 1, :], tv[:, :, r - 1, :], ALU.add), reads=[off_t, tot_t], writes=[off_t])
    P.op("dve", TT(cum[:, :, :], cum[:, :, :], offB[:, :, :], ALU.add), reads=[cum_t, off_t], writes=[cum_t])
    for i in range(NSLOT):
        bv, b_t = biasS[i]
        nk = 8 * i + 9
        cin = offsb[:, 2 * i:2 * i + 1, :].broadcast_to([128, nk, 8])
        P.op("dve", TT(bv[:, :, :], cum[:, 0:nk, :], cin, ALU.subtract), reads=[cum_t, osb_t], writes=[b_t])
        P.op("dve", TS(bv[:, 8 * i + 5:8 * i + 9, :], bv[:, 8 * i + 5:8 * i + 9, :], flg[:, 0:1], ALU.add),
             reads=[b_t, flg_t], writes=[b_t])
    dsh = [arena.tile([128, 4, 8], F32, "dsh%d" % i) for i in range(2)]
    shs = [arena.tile([8, 512], BF16, "shs%d" % i) for i in range(2)]
    ones8, ones8_t = arena.tile([8, LP], BF16, "ones8")
    P.op("dve", MS(ones8, 1.0), writes=[ones8_t])
    P.dma("pool", KTf_scr[:, 64, :], ones8, ones8_t, reads=[ones8_t])
    for i in range(NSLOT):
        dv_, d_t = dsh[i % 2]
        b0 = 1 + 8 * i
        cin = offsb[:, 2 * i:2 * i + 1, :].broadcast_to([128, 4, 8])
        P.op("dve", TT(dv_[:, :, :], cin, cum[:, b0:b0 + 4, :], ALU.subtract), reads=[cum_t, osb_t], writes=[d_t])
        ba = next_bank()
        for r in range(4):
            P.op("pe", MM(psF[0:8, ba, r * 128:(r + 1) * 128], dv_[:, r, :], identF, True, True),
                 reads=[d_t, cf_t], writes=[bankT[ba]])
        sv, s_t = shs[i % 2]
        P.op("act", ACTF(sv[:, :], psF[0:8, ba, :], AF.Copy), reads=[bankT[ba]], writes=[s_t])
        P.dma("pool", QTf_scr[:, 64, i * 512:(i + 1) * 512], sv[:, :], s_t, reads=[s_t])
    lamt, lam_t = arena.tile([128, 4, 64], F32, "lamt")
    lamp, lamp_t = arena.tile([128, 2, 64], F32, "lamp")
    lams, lams_t = arena.tile([128, 4], F32, "lams")
    P.dma("sp", lamt.rearrange("p a b -> p (a b)"), lam_d.rearrange("a b -> (a b)").partition_broadcast(128), lam_t, writes=[lam_t])
    P.dma("sp", subg, subg_d.rearrange("(p o) -> p o", o=1), subg_t, writes=[subg_t])
    P.op("dve", MS(lams[:, :], 0.0), writes=[lams_t])
    for j in range(2):
        P.op("dve", STT(lamp[:, j, :], lamt[:, 2 * j, :], 1.0, lamt[:, 2 * j + 1, :], ALU.mult, ALU.mult, accum=lams[:, j:j + 1]),
             reads=[lam_t, lams_t], writes=[lamp_t, lams_t])
    P.op("act", ACTF(lams[:, 2:4], lams[:, 0:2], AF.Exp), reads=[lams_t], writes=[lams_t])
    P.op("dve", STT(neglam[:, :], lams[:, 3:4], -LAMBDA_INIT, lams[:, 2:3], ALU.add, ALU.subtract), reads=[lams_t], writes=[nl_t])
    P.op("dve", TS(subg[:, :], subg[:, :], (1.0 - LAMBDA_INIT), ALU.mult), reads=[subg_t], writes=[subg_t])
    P.barrier()
    arena.off = b_mark

    ubuf = []
    for i in range(2):
        ubuf.append(dict(KT=arena.tile([128, LP], BF16, "KT%d" % i), VA=arena.tile([128, NBLK, 130], BF16, "VA%d" % i),
                         QT=arena.tile([128, 2, 4096], BF16, "QT%d" % i)))
    for ub in ubuf:
        qv_, q_t_ = ub["QT"]
        P.op("dve", MS(qv_[64:128, 0, :], 0.0), writes=[q_t_])
        P.op("dve", MS(qv_[0:64, 1, :], 0.0), writes=[q_t_])
    Pt = [arena.tile([128, 512], BF16, "Pt%d" % i) for i in range(4)]
    onesb, onesb_t = arena.tile([128, 2, 128], BF16, "onesb")
    P.op("dve", MS(onesb, 1.0), writes=[onesb_t])
    P.op("dve", MS(onesb[0:NPAD, 1, :], 0.0), writes=[onesb_t])
    o1n, o1n_t = arena.tile([128, 512], F32, "o1n")
    uu, uu_t = arena.tile([128, 512], F32, "uu")
    usq, usq_t = arena.tile([128, 512], F32, "usq")
    rdb, rdb_t = arena.tile([128, 512], F32, "rdb")
    lnb, lnb_t = arena.tile([128, 512], F32, "lnb")
    rsb, rsb_t = arena.tile([128, 512], F32, "rsb")
    oTs = [arena.tile([128, 512], BF16, "oTs%d" % i) for i in range(2)]
    recf, recf_t = arena.tile([128, 512], F32, "recf")
    bcs, bcs_t = arena.tile([64, 512], F32, "bcs")
    accT = [T("acc0"), T("acc1")]
    psBf = psB[:, :].bitcast(F32)

    units = [("d", m) for m in range(4)] + [("f", h) for h in range(8)]

    def unit_load(ui):
        kind, idx = units[ui]
        ub = ubuf[ui % 2]
        (ktv, kt_t), (vav, va_t), (qtv, qt_t) = ub["KT"], ub["VA"], ub["QT"]
        if kind == "d":
            P.dma("sp", ktv[:, :], KTd_scr[idx], kt_t, writes=[kt_t])
            P.dma("sp", qtv[0:64, 0, :], QTd_scr[idx][0:64, :], qt_t, writes=[qt_t])
            P.dma("sp", qtv[64:128, 1, :], QTd_scr[idx][64:128, :], qt_t, writes=[qt_t])
            src = Vd_scr[:, idx, :].rearrange("(b k) n -> k b n", k=128)
            P.dma("sp", vav[:, :, :], src, va_t, writes=[va_t])
        else:
            P.dma("sp", ktv[0:65, :], KTf_scr[idx], kt_t, writes=[kt_t])
            P.dma("sp", qtv[0:65, 0, :], QTf_scr[idx], qt_t, writes=[qt_t])
            src = Vf_scr[:, idx, :].rearrange("(b k) n -> k b n", k=128)
            P.dma("sp", vav[:, :, 0:66], src, va_t, writes=[va_t])

    cnt = {"t": 0, "acc": 0, "ev": 0}

    def run_unit(ui):
        kind, idx = units[ui]
        ub = ubuf[ui % 2]
        (ktv, kt_t), (vav, va_t), (qtv, qt_t) = ub["KT"], ub["VA"], ub["QT"]
        dv = 128 if kind == "d" else 64
        maps = [0, 1] if kind == "d" else [0]
        tiles = []
        for i in range(NSLOT):
            for mp in maps:
                nk = 8 * i + 9
                for kb in range(nk):
                    diag = (8 * i + 1 <= kb <= 8 * i + 4)
                    other = kb > 8 * i + 4
                    rel = kb - (8 * i + 1) if diag else 0
                    tiles.append(dict(i=i, mp=mp, kb=kb, rel=rel, diag=diag, other=other, first=(kb == 0), last=(kb == nk - 1)))
        ntl = len(tiles)
        state = {}

        def emit_qk(tl):
            k = cnt["t"]
            cnt["t"] += 1
            bank = k % 3
            tl["pb"] = k % 4
            i, mp, kb = tl["i"], tl["mp"], tl["kb"]
            c0 = tl["rel"] * 128
            rows = slice(0, 128) if kind == "d" else slice(0, 65)
            P.op("pe", MM(psF[:, bank, c0:512], ktv[rows, kb * 128:(kb + 1) * 128], qtv[rows, mp, i * 512 + c0:(i + 1) * 512], True, True),
                 reads=[kt_t, qt_t], writes=[bankT[bank]])
            pv, p_t = Pt[tl["pb"]]
            if kind == "f":
                bias = biasS[i][0][:, kb, idx:idx + 1]
                rd = [bankT[bank], biasS[i][1]]
            elif tl["other"]:
                bias = flg[:, 0:1]
                rd = [bankT[bank], flg_t]
            else:
                bias = 0.0
                rd = [bankT[bank]]
            P.op("act", ACTF(pv[:, c0:512], psF[:, bank, c0:512], AF.Exp, bias=bias, scale=1.0), reads=rd, writes=[p_t])
            if tl["diag"]:
                sub = pv[:, c0:c0 + 128]
                P.op("dve", STT(sub, sub, 3.0e38, tri, ALU.min, ALU.mult), reads=[p_t, cb_t], writes=[p_t])

        deferred = []

        def zero_acc(aset):
            if kind == "d":
                P.op("dve", MS(psF[:, 3 + 2 * aset, :], 0.0), writes=[accT[aset]])
                P.op("dve", MS(psF[:, 4 + 2 * aset, :], 0.0), writes=[accT[aset]])
            else:
                P.op("dve", MS(psF[0:65, 3 + aset, :], 0.0), writes=[accT[aset]])

        def emit_av(tl, step):
            i, mp, kb = tl["i"], tl["mp"], tl["kb"]
            if tl["first"]:
                if (i, mp) not in state:
                    state[(i, mp)] = cnt["acc"] % 2
                    cnt["acc"] += 1
                    zero_acc(state[(i, mp)])
            aset = state[(i, mp)]
            pv, p_t = Pt[tl["pb"]]
            r0 = tl["rel"] if tl["diag"] else 0
            c0 = r0 * 128
            if kind == "d":
                P.op("pe", MM(psF[:, 3 + 2 * aset, c0:512], vav[:, kb, 0:128], pv[:, c0:512], False, False, skip=True),
                     reads=[p_t, va_t, accT[aset]], writes=[accT[aset]])
                P.op("pe", MM(psF[:, 4 + 2 * aset, c0:512], onesb[:, 1 if kb == 0 else 0, :], pv[:, c0:512], False, False, skip=True),
                     reads=[p_t, onesb_t, accT[aset]], writes=[accT[aset]])
            else:
                P.op("pe", MM(psF[0:65, 3 + aset, c0:512], vav[:, kb, 0:65], pv[:, c0:512], False, False, skip=True),
                     reads=[p_t, va_t, accT[aset]], writes=[accT[aset]])
            if tl["last"]:
                nxt = tl.get("next")
                if nxt is not None:
                    state[nxt] = cnt["acc"] % 2
                    cnt["acc"] += 1
                    zero_acc(state[nxt])
                if kind == "d":
                    emit_evac(i, mp, aset, step)
                else:
                    emit_evac_fox(i, aset, step)

        def emit_evac_fox(i, aset, step):
            k = cnt["ev"]
            cnt["ev"] += 1
            ov_, o_t = oTs[k % 2]
            P.op("dve", RCP(recf[64:65, :], psF[64:65, 3 + aset, :]), reads=[accT[aset]], writes=[recf_t])

            def stage_b():
                P.op("pe", MM(psF[0:64, 5, :], onesF[64:65, 0:64], recf[64:65, :], True, True), reads=[recf_t, cf_t], writes=[bankT[5]])

            def stage_c():
                P.op("dve", CP(bcs[:, :], psF[0:64, 5, :]), reads=[bankT[5]], writes=[bcs_t])
                P.op("dve", TT(ov_[0:64, :], psF[0:64, 3 + aset, :], bcs[:, :], ALU.mult), reads=[accT[aset], bcs_t], writes=[o_t])
                ch = 4 + idx // 2
                p0 = (idx % 2) * 64
                P.dma("pool", oT_scr[ch][p0:p0 + 64, i * 512:(i + 1) * 512], ov_[0:64, :], o_t, reads=[o_t])

            deferred.append((step + 2, stage_b))
            deferred.append((step + 4, stage_c))

        def emit_evac(i, mp, aset, step):
            k = cnt["ev"]
            cnt["ev"] += 1
            acc = psF[:, 3 + 2 * aset, :]
            den = psF[:, 4 + 2 * aset, :]
            P.op("dve", RCP(rdb[:, :], den), reads=[accT[aset]], writes=[rdb_t])
            if mp == 0:
                P.op("dve", TT(o1n[:, :], acc, rdb[:, :], ALU.mult), reads=[accT[aset], rdb_t], writes=[o1n_t])
                return
            P.op("dve", TT(uu[:, :], acc, rdb[:, :], ALU.mult), reads=[accT[aset], rdb_t], writes=[uu_t])
            P.op("dve", STT(uu[:, :], uu[:, :], neglam[:, 0:1], o1n[:, :], ALU.mult, ALU.add), reads=[uu_t, nl_t, o1n_t], writes=[uu_t])
            P.op("dve", TT(usq[:, :], uu[:, :], uu[:, :], ALU.mult), reads=[uu_t], writes=[usq_t])
            ov_, o_t = oTs[k % 2]

            def stage_b():
                P.op("pe", MM(psBf, onesF, usq[:, :], True, True), reads=[usq_t, cf_t], writes=[psBT])
                P.op("act", ACTF(lnb[:, :], psBf, AF.Ln, bias=1e-5, scale=1.0 / 128), reads=[psBT], writes=[lnb_t])
                P.op("act", ACTF(rsb[:, :], lnb[:, :], AF.Exp, scale=-0.5), reads=[lnb_t], writes=[rsb_t])

            def stage_c():
                P.op("dve", STT(ov_[:, :], uu[:, :], subg[:, 0:1], rsb[:, :], ALU.mult, ALU.mult), reads=[uu_t, subg_t, rsb_t], writes=[o_t])
                P.dma("pool", oT_scr[idx][:, i * 512:(i + 1) * 512], ov_[:, :], o_t, reads=[o_t])

            deferred.append((step + 3, stage_b))
            deferred.append((step + 6, stage_c))

        for t in range(ntl):
            if tiles[t]["last"] and t + 1 < ntl:
                tiles[t]["next"] = (tiles[t + 1]["i"], tiles[t + 1]["mp"])
        LAG = 2
        for t in range(ntl + LAG):
            if t < ntl:
                emit_qk(tiles[t])
            if t >= LAG:
                emit_av(tiles[t - LAG], t)
            while deferred and deferred[0][0] <= t:
                deferred.pop(0)[1]()
        while deferred:
            deferred.pop(0)[1]()

    unit_load(0)
    for ui in range(len(units)):
        if ui + 1 < len(units):
            unit_load(ui + 1)
        if ui == 4:
            P.barrier()
        run_unit(ui)
    end_phase()

    Wo = arena.tile([128, 8, D], BF16, "Wo")
    stg = [arena.tile([128, 1024], F32, "stgc%d" % i) for i in range(3)]
    Wo_tk = load_weights(w_out, D, D, Wo, None, stg)
    Wov, Wo_t = Wo
    oTb = [arena.tile([128, 8, 512], BF16, "oTb%d" % i) for i in range(2)]
    hb = [arena.tile([128, D], F32, "hb%d" % i) for i in range(6)]

    def c1_loads(i):
        ov_, o_t = oTb[i % 2]
        P.dma("sp", ov_[:, :, :], oT_scr[:, :, i * 512:(i + 1) * 512].rearrange("c p q -> p c q"), o_t, writes=[o_t])

    def c1_block(i, r, hk):
        ov_, o_t = oTb[i % 2]
        bv, b_t = hb[hk % 6]
        rows = slice(i * 512 + r * 128, i * 512 + (r + 1) * 128)
        P.dma("sp", bv[:, :], h1_scr[rows, :], b_t, writes=[b_t])
        for half in range(2):
            bank = next_bank()
            for c in range(8):
                P.op("pe", MM(psF[:, bank, :], ov_[:, c, r * 128:(r + 1) * 128], Wov[:, c, half * 512:(half + 1) * 512], c == 0, c == 7),
                     reads=[o_t, Wo_tk[0]], writes=[bankT[bank]])
            o = bv[:, half * 512:(half + 1) * 512]
            P.op("dve", TT(o, psF[:, bank, :], o, ALU.add), reads=[bankT[bank], b_t], writes=[b_t])
        P.dma("pool", h1_scr[rows, :], bv[:, :], b_t, reads=[b_t])

    c1_loads(0)
    hk = 0
    for i in range(NSLOT):
        if i + 1 < NSLOT:
            c1_loads(i + 1)
        for r in range(4):
            c1_block(i, r, hk)
            hk += 1
    end_phase()

    own_groups = [(i * 512, 4, i) for i in range(NSLOT)]

    def c2_extra():
        ex = dict(gf=arena.tile([128, D], F32, "gf"), ss3=arena.tile([128, 4], F32, "ss3"),
                  ln3=arena.tile([128, 4], F32, "ln3"), rs3=arena.tile([128, 4], F32, "rs3"), k=[0])
        gv, g_t = ex["gf"]
        P.dma("sp", gv[:, :], gfin.partition_broadcast(128), g_t, writes=[g_t])
        return ex

    def c2_load(g, r, buf):
        tok0, nb, own = g
        bv, b_t = buf
        P.dma("sp", bv[:, :], h1_scr[tok0 + r * 128: tok0 + (r + 1) * 128, :], b_t, writes=[b_t])

    def c2_post(g, r, buf, ex):
        tok0, nb, own = g
        bv, b_t = buf
        k = ex["k"][0]
        ex["k"][0] += 1
        col = k % 4
        ssv, ss_t = ex["ss3"]
        jv, j_t = ex["junk"]
        lv, l_t = ex["ln3"]
        rv, r_t = ex["rs3"]
        gv, g_t = ex["gf"]
        sumsq(bv[:, :], b_t, jv[:, :], j_t, ssv[:, col:col + 1], ss_t)
        rstd_chain(ssv[:, col:col + 1], ss_t, lv[:, col:col + 1], l_t, rv[:, col:col + 1], r_t, 1.0 / D, 1e-6)
        P.op("dve", STT(bv[:, :], bv[:, :], rv[:, col:col + 1], gv[:, :], ALU.mult, ALU.mult), reads=[b_t, r_t, g_t], writes=[b_t])
        P.dma("pool", out_d[tok0 + r * 128: tok0 + (r + 1) * 128, :], bv[:, :], b_t, reads=[b_t])

    ffn_phase(w_gu2, w_d2, gc[:, 16:24], own_groups, c2_load, c2_post, c2_extra)

    with nc.Block() as block:
        @block.tensor
        def _(e):
            for f in P.streams["pe"]:
                f(e)

        @block.scalar
        def _(e):
            for f in P.streams["act"]:
                f(e)

        @block.vector
        def _(e):
            for f in P.streams["dve"]:
                f(e)

        @block.gpsimd
        def _(e):
            for f in P.streams["pool"]:
                f(e)

        @block.sync
        def _(e):
            for f in P.streams["sp"]:
                f(e)
    return nc, P


def _host_constants():
    ident = np.eye(128, dtype=np.float32)
    tri = np.triu(np.ones((128, 128), dtype=np.float32))
    cbf = np.concatenate([ident, tri], axis=1).astype(ml_dtypes.bfloat16)
    cf32 = np.concatenate([tri, np.ones((128, 128), np.float32), ident], axis=1).astype(np.float32)
    return cbf, cf32


def _rope_tables(pos, qscale):
    inv_freq = np.power(np.float32(500000.0), -np.arange(0, 16, 2, dtype=np.float32) / np.float32(16)).astype(np.float32)
    ang = (pos.astype(np.float32)[None, :] * inv_freq[:, None]).astype(np.float32)
    c = np.cos(ang).astype(np.float32)
    s = np.sin(ang).astype(np.float32)
    n = pos.shape[0]
    cosT = np.ones((64, n), np.float32)
    sinT = np.zeros((64, n), np.float32)
    cosT[0:8] = c
    cosT[8:16] = c
    sinT[0:8] = -s
    sinT[8:16] = s
    cosT = np.tile(cosT, (2, 1)) * np.float32(qscale)
    sinT = np.tile(sinT, (2, 1)) * np.float32(qscale)
    return np.ascontiguousarray(cosT), np.ascontiguousarray(sinT)


_CACHE = {}


def kernel(x, meta_tokens, ffn1_norm_g, ffn1_w_gate_up, ffn1_w_down, mix_norm_g, w_in, b_forget,
           lam_q1, lam_k1, lam_q2, lam_k2, diff_subln_g, w_out, ffn2_norm_g, ffn2_w_gate_up,
           ffn2_w_down, final_norm_g):
    f32 = np.float32
    x = np.asarray(x, f32)
    B = x.shape[0]
    if "nc" not in _CACHE:
        _CACHE["nc"] = build_program()[0]
    nc = _CACHE["nc"]
    cbf, cf32 = _host_constants()
    w_in0 = np.ascontiguousarray(np.asarray(w_in, f32)[0])
    perm = np.arange(1024)
    for base in range(0, 1024, 64):
        perm[base:base + 8] = np.arange(base + 8, base + 16)
        perm[base + 8:base + 16] = np.arange(base, base + 8)
    w_sw = np.ascontiguousarray(w_in0[:, perm])
    gcols = np.concatenate([np.asarray(g, f32)[0].reshape(8, 128).T for g in (ffn1_norm_g, mix_norm_g, ffn2_norm_g)], axis=1)
    gcols = np.ascontiguousarray(gcols)
    lam = np.ascontiguousarray(np.stack([np.asarray(v, f32)[0] for v in (lam_q1, lam_k1, lam_q2, lam_k2)]))
    shared = dict(
        w_gu1=np.ascontiguousarray(np.asarray(ffn1_w_gate_up, f32)[0]), w_d1=np.ascontiguousarray(np.asarray(ffn1_w_down, f32)[0]),
        w_gu2=np.ascontiguousarray(np.asarray(ffn2_w_gate_up, f32)[0]), w_d2=np.ascontiguousarray(np.asarray(ffn2_w_down, f32)[0]),
        w_in=w_in0, w_sw=w_sw, w_out=np.ascontiguousarray(np.asarray(w_out, f32)[0]), gcols=gcols,
        gfin=np.ascontiguousarray(np.asarray(final_norm_g, f32)), bfg=np.ascontiguousarray(np.asarray(b_forget, f32)[0]),
        lam=lam, subg=np.ascontiguousarray(np.asarray(diff_subln_g, f32)[0]), cbf=cbf, cf32=cf32)
    in_maps = []
    meta = np.asarray(meta_tokens, f32)
    for core in range(8):
        b, c = core // 2, core % 2
        xa = np.zeros((LP, D), f32)
        xa[NPAD:128] = meta
        pos = np.zeros(LP, f32)
        pos[0:128] = np.arange(128) - NPAD
        posq = np.zeros(4096, f32)
        for j in range(16):
            i = j // 2
            a = 2 * i + c if j % 2 == 0 else 2 * i + 1 - c
            xa[128 + 512 * j: 128 + 512 * (j + 1)] = x[b, 512 * a: 512 * (a + 1)]
            p = 128 + 512 * a + np.arange(512) - NPAD
            pos[128 + 512 * j: 128 + 512 * (j + 1)] = p
            if j % 2 == 0:
                posq[512 * i: 512 * (i + 1)] = p
        cK, sK = _rope_tables(pos, 1.0)
        cQ, sQ = _rope_tables(posq, 0.125)
        fl = np.zeros((128, 4), f32)
        fl[:, 0] = 0.0 if c == 1 else -30000.0
        fl[:, 1] = float(c)
        fl[:, 2] = float(1 - c)
        m = dict(shared)
        m.update(x_arr=xa, cosK=cK, sinK=sK, cosQ=cQ, sinQ=sQ, flags=fl)
        in_maps.append(m)
    res = run_bass_kernel_spmd(nc, in_maps, core_ids=list(range(8)))
    out = np.zeros((B, SEQ, D), f32)
    for core in range(8):
        b, c = core // 2, core % 2
        o = np.asarray(res.results[core]["out"])
        for i in range(NSLOT):
            a = 2 * i + c
            out[b, 512 * a: 512 * (a + 1)] = o[512 * i: 512 * (i + 1)]
    return out
```

```python
import math
import numpy as np
import ml_dtypes
import concourse.bass as bass
import concourse.mybir as mybir
from concourse.bass_utils import run_bass_kernel_spmd

F32 = mybir.dt.float32
BF16 = mybir.dt.bfloat16
AF = mybir.ActivationFunctionType
ALU = mybir.AluOpType

D = 1024
DFF = 2816
NF = DFF // 128
SEQ = 8192
NMETA = 16
NPAD = 112
LP = 8320
NBLK = 65
NSLOT = 8
INC = 3080
LAMBDA_INIT = 0.8 - 0.6 * math.exp(-0.3 * 0)
ENG = ["pe", "act", "dve", "pool", "sp"]


def MM(out, lhsT, rhs, start, stop, skip=False):
    if skip:
        return lambda e: e.matmul(out, lhsT=lhsT, rhs=rhs, start=start, stop=stop, skip_group_check=True)
    return lambda e: e.matmul(out, lhsT=lhsT, rhs=rhs, start=start, stop=stop)


def TR(out, in_, idt):
    return lambda e: e.transpose(out, in_, idt)


def ACTF(out, in_, func, bias=None, scale=None):
    kw = {}
    if bias is not None:
        kw["bias"] = bias
    if scale is not None:
        kw["scale"] = scale
    return lambda e: e.activation(out=out, in_=in_, func=func, **kw)


def AMUL(out, in_, m):
    return lambda e: e.mul(out, in_, m)


def TT(out, in0, in1, op):
    return lambda e: e.tensor_tensor(out=out, in0=in0, in1=in1, op=op)


def TS(out, in0, s1, op0):
    return lambda e: e.tensor_scalar(out=out, in0=in0, scalar1=s1, scalar2=None, op0=op0)


def STT(out, in0, scalar, in1, op0, op1, accum=None):
    if accum is not None:
        return lambda e: e.scalar_tensor_tensor(out=out, in0=in0, scalar=scalar, in1=in1, op0=op0, op1=op1, accum_out=accum)
    return lambda e: e.scalar_tensor_tensor(out=out, in0=in0, scalar=scalar, in1=in1, op0=op0, op1=op1)


def CP(out, in_):
    return lambda e: e.tensor_copy(out=out, in_=in_)


def MS(ap, val):
    return lambda e: e.memset(ap, val)


def RCP(out, in_):
    return lambda e: e.reciprocal(out=out, in_=in_)


class Sem:
    __slots__ = ("h", "count", "dma")

    def __init__(self, nc, name, dma=False):
        self.h = nc.alloc_semaphore(name)
        self.count = 0
        self.dma = dma


class T:
    __slots__ = ("w", "r", "dsem", "name")

    def __init__(self, name=""):
        self.w = None
        self.r = {}
        self.dsem = {}
        self.name = name


class Prog:
    def __init__(self, nc):
        self.nc = nc
        self.streams = {e: [] for e in ENG}
        self.esem = {e: Sem(nc, "e_" + e) for e in ENG}
        self.waited = {e: {} for e in ENG}
        self.dsems = []
        self.free_dsems = {"sp": [], "pool": []}
        self.ninst = 0

    def _deps(self, eng, reads, writes):
        deps = {}
        own = self.esem[eng]

        def add(sv, is_w):
            if sv is None:
                return
            s, v = sv
            if s is own and (eng == "pe" or not is_w):
                return
            if s.dma:
                v = s.count
            if deps.get(s, 0) < v:
                deps[s] = v

        for t in reads:
            add(t.w, True)
        for t in writes:
            add(t.w, True)
            for s, v in t.r.items():
                add((s, v), False)
        out = []
        w = self.waited[eng]
        for s, v in deps.items():
            if w.get(s, 0) >= v:
                continue
            w[s] = v
            out.append((s.h, v))
        return out

    def op(self, eng, fn, reads=(), writes=()):
        waits = self._deps(eng, reads, writes)
        sem = self.esem[eng]
        sem.count += 1
        val = sem.count
        h = sem.h

        def emit(e):
            for sh, v in waits:
                e.wait_ge(sh, v)
            fn(e).then_inc(h, 1)

        self.streams[eng].append(emit)
        self.ninst += 1
        for t in writes:
            t.w = (sem, val)
            t.r = {}
        for t in reads:
            t.r[sem] = val

    def get_dsem(self, t, q):
        if q not in t.dsem:
            if self.free_dsems[q]:
                t.dsem[q] = self.free_dsems[q].pop()
            else:
                s = Sem(self.nc, "d%s%d" % (q, len(self.dsems)), dma=True)
                self.dsems.append(s)
                t.dsem[q] = s
        return t.dsem[q]

    def dma(self, q, out_ap, in_ap, st, reads=(), writes=()):
        waits = self._deps(q, reads, writes)
        ds = self.get_dsem(st, q)
        ds.count += 16
        val = ds.count
        h = ds.h

        def emit(e):
            for sh, v in waits:
                e.wait_ge(sh, v)
            e.dma_start(out=out_ap, in_=in_ap).then_inc(h, 16)

        self.streams[q].append(emit)
        self.ninst += 1
        for t in writes:
            t.w = (ds, val)
            t.r = {}
        for t in reads:
            t.r[ds] = val

    def barrier(self, release=()):
        sems = [s for s in list(self.esem.values()) + self.dsems if s.count > 0]
        for e in ENG:
            waits = []
            w = self.waited[e]
            for s in sems:
                if s is self.esem[e]:
                    continue
                if w.get(s, 0) >= s.count:
                    continue
                w[s] = s.count
                waits.append((s.h, s.count))
            if waits:
                def emit(en, waits=waits):
                    for sh, v in waits:
                        en.wait_ge(sh, v)
                self.streams[e].append(emit)
        for t in release:
            for q, sm in t.dsem.items():
                self.free_dsems[q].append(sm)
            t.dsem = {}


class Arena:
    def __init__(self, nc, nbytes):
        self.n32 = nbytes // 4
        self.h = nc.alloc_sbuf_tensor("arena", [128, self.n32], F32)
        self.off = 0
        self.mark_ = 0
        self.tokens = []

    def tile(self, shape, dtype, name=""):
        free = 1
        for s in shape[1:]:
            free *= s
        nb = free * (4 if dtype == F32 else 2)
        nb = (nb + 31) // 32 * 32
        assert self.off + nb <= self.n32 * 4, ("arena overflow", name, self.off, nb, self.n32 * 4)
        v = self.h[:, self.off // 4:(self.off + nb) // 4]
        if dtype != F32:
            v = v.bitcast(dtype)
        v = v[:, 0:free]
        if len(shape) == 3:
            v = v.rearrange("p (a b) -> p a b", a=shape[1])
        elif len(shape) == 4:
            v = v.rearrange("p (a b c) -> p a b c", a=shape[1], b=shape[2])
        if shape[0] < 128:
            v = v[0:shape[0]]
        self.off += nb
        t = T(name)
        self.tokens.append(t)
        return v, t

    def mark(self):
        self.mark_ = self.off
        self.tokens = []

    def reset(self):
        self.off = self.mark_
        toks = self.tokens
        self.tokens = []
        return toks


def build_program():
    nc = bass.Bass("TRN2", target_bir_lowering=False)
    P = Prog(nc)

    def din(name, shape, dt=F32):
        return nc.dram_tensor(name, list(shape), dt, kind="ExternalInput").ap()

    def dscr(name, shape, dt):
        return nc.dram_tensor(name, list(shape), dt, kind="Internal").ap()

    x_arr = din("x_arr", [LP, D])
    w_gu1 = din("w_gu1", [D, 2 * DFF])
    w_d1 = din("w_d1", [DFF, D])
    w_gu2 = din("w_gu2", [D, 2 * DFF])
    w_d2 = din("w_d2", [DFF, D])
    w_in = din("w_in", [D, INC])
    w_sw = din("w_sw", [D, 1024])
    w_out = din("w_out", [D, D])
    gcols = din("gcols", [128, 24])
    gfin = din("gfin", [D])
    bfg_d = din("bfg", [8])
    lam_d = din("lam", [4, 64])
    subg_d = din("subg", [128])
    cosK = din("cosK", [128, LP])
    sinK = din("sinK", [128, LP])
    cosQ = din("cosQ", [128, 4096])
    sinQ = din("sinQ", [128, 4096])
    cbf = din("cbf", [128, 256], BF16)
    cf32 = din("cf32", [128, 384])
    flags = din("flags", [128, 4])
    out_d = nc.dram_tensor("out", [4096, D], F32, kind="ExternalOutput").ap()

    h1_scr = dscr("h1_scr", [4096, D], F32)
    hnT_scr = dscr("hnT_scr", [128, 8, LP], BF16)
    KTd_scr = dscr("KTd_scr", [4, 128, LP], BF16)
    KTf_scr = dscr("KTf_scr", [8, 65, LP], BF16)
    Vd_scr = dscr("Vd_scr", [LP, 4, 130], BF16)
    Vf_scr = dscr("Vf_scr", [LP, 8, 66], BF16)
    QTd_scr = dscr("QTd_scr", [4, 128, 4096], BF16)
    QTf_scr = dscr("QTf_scr", [8, 65, 4096], BF16)
    oT_scr = dscr("oT_scr", [8, 128, 4096], BF16)

    arena = Arena(nc, (nc.sbuf_bytes_remaining - 256) // 32 * 32)
    psF = nc.alloc_psum_tensor("psF", [128, 7, 512], F32)
    psB = nc.alloc_psum_tensor("psB", [128, 1024], BF16)
    bankT = [T("bank%d" % i) for i in range(7)]
    psBT = T("psB")
    psB3 = psB[:, :].rearrange("p (c t) -> p c t", c=8)

    cb, cb_t = arena.tile([128, 256], BF16, "cbf")
    ident = cb[:, 0:128]
    tri = cb[:, 128:256]
    cf, cf_t = arena.tile([128, 384], F32, "cf32")
    triF = cf[:, 0:128]
    onesF = cf[:, 128:256]
    identF = cf[:, 256:384]
    gc, gc_t = arena.tile([128, 24], F32, "gcols")
    flg, flg_t = arena.tile([128, 4], F32, "flags")
    bfg, bfg_t = arena.tile([128, 8], F32, "bfg")
    lpos, lpos_t = arena.tile([128, NBLK, 8], F32, "lpos")
    P.dma("sp", cb, cbf, cb_t, writes=[cb_t])
    P.dma("sp", cf, cf32, cf_t, writes=[cf_t])
    P.dma("sp", gc, gcols, gc_t, writes=[gc_t])
    P.dma("sp", flg, flags, flg_t, writes=[flg_t])
    P.dma("sp", bfg, bfg_d.partition_broadcast(128), bfg_t, writes=[bfg_t])
    base_mark = arena.off
    arena.tokens = []

    def end_phase():
        arena.off = base_mark
        toks = arena.tokens
        arena.tokens = []
        P.barrier(release=toks)

    groups = [(0, 1, None)]
    for j in range(16):
        groups.append((128 + 512 * j, 4, (j // 2) if j % 2 == 0 else None))

    rr = {"cast": 0, "bank": 0}

    def next_bank():
        b = rr["bank"] % 7
        rr["bank"] += 1
        return b

    def load_weights(wd, K, N, dst, scale_cols, stg, order=None):
        dstv, dst_t = dst
        nblk = (N + 1023) // 1024
        toks = [T("wblk%d" % i) for i in range(nblk)]
        for bi in (order if order is not None else range(nblk)):
            n0 = bi * 1024
            w = min(1024, N - n0)
            for c in range(K // 128):
                sb, sb_t = stg[rr["cast"] % len(stg)]
                P.dma("sp", sb[:, 0:w], wd[c * 128:(c + 1) * 128, n0:n0 + w], sb_t, writes=[sb_t])
                o = dstv[:, c, n0:n0 + w]
                i = sb[:, 0:w]
                if scale_cols is not None:
                    sc = scale_cols[:, c:c + 1]
                    if rr["cast"] % 2 == 0:
                        P.op("act", AMUL(o, i, sc), reads=[sb_t, gc_t], writes=[toks[bi]])
                    else:
                        P.op("dve", TS(o, i, sc, ALU.mult), reads=[sb_t, gc_t], writes=[toks[bi]])
                else:
                    if rr["cast"] % 2 == 0:
                        P.op("act", ACTF(o, i, AF.Copy), reads=[sb_t], writes=[toks[bi]])
                    else:
                        P.op("dve", CP(o, i), reads=[sb_t], writes=[toks[bi]])
                rr["cast"] += 1
        return toks

    def rstd_chain(ss_ap, ss_t, ln_ap, ln_t, rs_ap, rs_t, inv_d, eps):
        P.op("act", ACTF(ln_ap, ss_ap, AF.Ln, bias=eps, scale=inv_d), reads=[ss_t], writes=[ln_t])
        P.op("act", ACTF(rs_ap, ln_ap, AF.Exp, scale=-0.5), reads=[ln_t], writes=[rs_t])

    def sumsq(src_ap, src_t, junk_ap, junk_t, acc_ap_, acc_t):
        P.op("dve", MS(acc_ap_, 0.0), writes=[acc_t])
        P.op("dve", STT(junk_ap, src_ap, 1.0, src_ap, ALU.mult, ALU.mult, accum=acc_ap_),
             reads=[src_t, acc_t], writes=[junk_t, acc_t])

    def transpose8(src_ap, src_t, dst_ap, dst_t):
        for c in range(8):
            P.op("pe", TR(psB[:, c * 128:(c + 1) * 128], src_ap[:, c * 128:(c + 1) * 128], ident),
                 reads=[src_t, cb_t], writes=[psBT])
        P.op("dve", CP(dst_ap, psB3), reads=[psBT], writes=[dst_t])

    def ffn_phase(w_gu, w_d, gcol, grp_list, load_block, post_block, extra_alloc):
        Wgu = arena.tile([128, 8, 2 * DFF], BF16, "Wgu")
        Wd = arena.tile([128, NF, D], BF16, "Wd")
        xa = [arena.tile([128, D], F32, "xa%d" % i) for i in range(2)]
        xd = [arena.tile([128, D], F32, "xd%d" % i) for i in range(3)]
        xn = arena.tile([128, D], BF16, "xn")
        xnT = [arena.tile([128, 8, 512], BF16, "xnT%d" % i) for i in range(2)]
        hT = arena.tile([128, NF, 512], BF16, "hT")
        sg = arena.tile([128, 512], F32, "sg")
        ss = arena.tile([128, 4], F32, "ss")
        lnv = arena.tile([128, 4], F32, "lnv")
        rstd = arena.tile([128, 4], F32, "rstd")
        extra = extra_alloc()
        extra["junk"] = xn
        extra["hn"] = xn
        Wgu_tk = load_weights(w_gu, D, 2 * DFF, Wgu, gcol, xa + xd, order=[0, 2, 3, 1, 4, 5])
        Wd_tk = load_weights(w_d, DFF, D, Wd, None, xa + xd)
        Wguv, Wgu_t = Wgu
        Wdv, Wd_t = Wd
        hTv, hT_t = hT
        xnv, xn_t = xn
        sgv, sg_t = sg
        ssv, ss_t = ss
        lnvv, lnv_t = lnv
        rsv, rs_t = rstd
        ctr = {"n": 0, "d": 0}
        ngrp = len(grp_list)
        nbuf = {}

        def stage_n_load(gi, r):
            k = ctr["n"]
            ctr["n"] += 1
            buf = xa[k % 2]
            load_block(grp_list[gi], r, buf)
            nbuf[(gi, r)] = (buf, k % 4)

        def stage_n_chain(gi, r):
            (bv, b_t), col = nbuf[(gi, r)]
            sumsq(bv[:, :], b_t, xnv[:, :], xn_t, ssv[:, col:col + 1], ss_t)
            rstd_chain(ssv[:, col:col + 1], ss_t, lnvv[:, col:col + 1], lnv_t, rsv[:, col:col + 1], rs_t, 1.0 / D, 1e-6)
            P.op("act", AMUL(xnv[:, :], bv[:, :], rsv[:, col:col + 1]), reads=[b_t, rs_t], writes=[xn_t])

        def stage_n_tr(gi, r):
            tv_, t_t = xnT[gi % 2]
            transpose8(xnv, xn_t, tv_[:, :, r * 128:(r + 1) * 128], t_t)

        def stage_gu(gi, f):
            tok0, nb, own = grp_list[gi]
            Tn = nb * 128
            tv_, t_t = xnT[gi % 2]
            pg = (rr["bank"] % 2) * 2
            rr["bank"] += 1
            for half, bank in ((0, pg), (1, pg + 1)):
                col = half * DFF + f * 128
                for c in range(8):
                    P.op("pe", MM(psF[:, bank, 0:Tn], Wguv[:, c, col:col + 128], tv_[:, c, 0:Tn], c == 0, c == 7),
                         reads=[Wgu_tk[col // 1024], t_t], writes=[bankT[bank]])
            P.op("act", ACTF(sgv[:, 0:Tn], psF[:, pg, 0:Tn], AF.Silu), reads=[bankT[pg]], writes=[sg_t])
            P.op("dve", TT(hTv[:, f, 0:Tn], sgv[:, 0:Tn], psF[:, pg + 1, 0:Tn], ALU.mult),
                 reads=[sg_t, bankT[pg + 1]], writes=[hT_t])

        dbuf = {}

        def d_load(gi, r):
            k = ctr["d"]
            ctr["d"] += 1
            buf = xd[k % 3]
            load_block(grp_list[gi], r, buf)
            dbuf[(gi, r)] = buf

        def stage_d(gi, r):
            bv, b_t = dbuf[(gi, r)]
            for half in range(2):
                bank = 4 + (rr["bank"] % 2)
                rr["bank"] += 1
                for f in range(NF):
                    P.op("pe", MM(psF[:, bank, :], hTv[:, f, r * 128:(r + 1) * 128], Wdv[:, f, half * 512:(half + 1) * 512],
                                  f == 0, f == NF - 1), reads=[hT_t, Wd_tk[0]], writes=[bankT[bank]])
                o = bv[:, half * 512:(half + 1) * 512]
                P.op("dve", STT(o, psF[:, bank, :], 0.5, o, ALU.mult, ALU.add), reads=[bankT[bank], b_t], writes=[b_t])

        for r in range(grp_list[0][1]):
            stage_n_load(0, r)
            stage_n_chain(0, r)
            stage_n_tr(0, r)
        pending = None
        for gi, g in enumerate(grp_list):
            tok0, nb, own = g
            nb_next = grp_list[gi + 1][1] if gi + 1 < ngrp else 0
            for f in range(NF):
                stage_gu(gi, f)
                if f == 1 and pending is not None:
                    pending()
                    pending = None
                r_, ph = divmod(f, 5)
                if r_ < nb_next:
                    if ph == 0:
                        stage_n_load(gi + 1, r_)
                    elif ph == 2:
                        stage_n_chain(gi + 1, r_)
                    elif ph == 4:
                        stage_n_tr(gi + 1, r_)
                if f == 16:
                    d_load(gi, 0)
                    if nb > 1:
                        d_load(gi, 1)
            for r in range(nb):
                stage_d(gi, r)
                if pending is not None:
                    pending()
                    pending = None
                pending = post_block(g, r, dbuf[(gi, r)], extra)
                if r + 2 < nb:
                    d_load(gi, r + 2)
        if pending is not None:
            pending()
        end_phase()

    def a1_extra():
        return dict(hnTs=arena.tile([128, 8, 128], BF16, "hnTs"),
                    ss2=arena.tile([128, 4], F32, "ss2"), ln2=arena.tile([128, 4], F32, "ln2"),
                    rs2=arena.tile([128, 4], F32, "rs2"), k=[0])

    def a1_load(g, r, buf):
        tok0, nb, own = g
        bv, b_t = buf
        P.dma("sp", bv[:, :], x_arr[tok0 + r * 128: tok0 + (r + 1) * 128, :], b_t, writes=[b_t])

    def a1_post(g, r, buf, ex):
        tok0, nb, own = g
        bv, b_t = buf
        if own is not None:
            P.dma("pool", h1_scr[own * 512 + r * 128: own * 512 + (r + 1) * 128, :], bv[:, :], b_t, reads=[b_t])
        k = ex["k"][0]
        ex["k"][0] += 1
        col = k % 4
        ssv, ss_t = ex["ss2"]
        lv, l_t = ex["ln2"]
        rv, r_t = ex["rs2"]
        hv, h_t = ex["hn"]
        sumsq(bv[:, :], b_t, hv[:, :], h_t, ssv[:, col:col + 1], ss_t)
        rstd_chain(ssv[:, col:col + 1], ss_t, lv[:, col:col + 1], l_t, rv[:, col:col + 1], r_t, 1.0 / D, 1e-6)
        P.op("act", AMUL(hv[:, :], bv[:, :], rv[:, col:col + 1]), reads=[b_t, r_t], writes=[h_t])

        def part2():
            sv, s_t = ex["hnTs"]
            transpose8(hv, h_t, sv[:, :, :], s_t)
            P.dma("pool", hnT_scr[:, :, tok0 + r * 128: tok0 + (r + 1) * 128], sv[:, :, :], s_t, reads=[s_t])
        return part2

    ffn_phase(w_gu1, w_d1, gc[:, 0:8], groups, a1_load, a1_post, a1_extra)

    Win = arena.tile([128, 8, INC], BF16, "Win")
    Wsw = arena.tile([128, 8, 1024], BF16, "Wsw")
    stg = [arena.tile([128, 1024], F32, "stg%d" % i) for i in range(3)]
    Win_tk = load_weights(w_in, D, INC, Win, gc[:, 8:16], stg, order=[0])
    Wsw_tk = load_weights(w_sw, D, 1024, Wsw, gc[:, 8:16], stg)
    Win_tk2 = load_weights(w_in, D, INC, Win, gc[:, 8:16], stg, order=[2, 1, 3])
    for _bi in (1, 2, 3):
        Win_tk[_bi] = Win_tk2[_bi]
    Winv, Win_t = Win
    Wswv, Wsw_t = Wsw
    hnTb = [arena.tile([128, 8, 512], BF16, "hnTb%d" % i) for i in range(2)]
    csK = [arena.tile([128, 2, 512], F32, "csK%d" % i) for i in range(2)]
    csQ = [arena.tile([128, 2, 512], F32, "csQ%d" % i) for i in range(2)]
    t1 = [arena.tile([128, 512], F32, "t1_%d" % i) for i in range(2)]
    t2 = [arena.tile([128, 512], F32, "t2_%d" % i) for i in range(2)]
    kst = [arena.tile([128, 512], BF16, "kst%d" % i) for i in range(4)]
    vdst = [arena.tile([128, 4, 130], BF16, "vdst%d" % i) for i in range(2)]
    vd0 = arena.tile([128, 4, 130], BF16, "vd0")
    vfst = [arena.tile([128, 8, 66], BF16, "vfst%d" % i) for i in range(2)]
    vf0 = arena.tile([128, 8, 66], BF16, "vf0")
    zt = [arena.tile([128, 8], F32, "zt%d" % i) for i in range(2)]
    et = [arena.tile([128, 8], F32, "et%d" % i) for i in range(2)]
    for (v, t) in vdst + vfst + [vd0, vf0]:
        P.op("dve", MS(v, 1.0), writes=[t])
    P.op("dve", MS(vd0[0][0:NPAD], 0.0), writes=[vd0[1]])
    P.op("dve", MS(vf0[0][0:NPAD], 0.0), writes=[vf0[1]])
    kk = {"k": 0, "v": 0}

    def rope_proj(col0, cs, cs_t, dst_scr, m, Tn, c0, hv, h_t):
        ba, bb = next_bank(), next_bank()
        for c in range(8):
            P.op("pe", MM(psF[:, ba, 0:Tn], Winv[:, c, col0 + m * 128: col0 + (m + 1) * 128], hv[:, c, 0:Tn], c == 0, c == 7),
                 reads=[Win_tk[(col0 + m * 128) // 1024], h_t], writes=[bankT[ba]])
        for c in range(8):
            P.op("pe", MM(psF[:, bb, 0:Tn], Wswv[:, c, col0 + m * 128: col0 + (m + 1) * 128], hv[:, c, 0:Tn], c == 0, c == 7),
                 reads=[Wsw_tk[0], h_t], writes=[bankT[bb]])
        k = kk["k"]
        kk["k"] += 1
        av, a_t = t1[k % 2]
        bv, b_t = t2[k % 2]
        sv, s_t = kst[k % 4]
        P.op("dve", TT(av[:, 0:Tn], psF[:, ba, 0:Tn], cs[:, 0, 0:Tn], ALU.mult), reads=[bankT[ba], cs_t], writes=[a_t])
        P.op("dve", TT(bv[:, 0:Tn], psF[:, bb, 0:Tn], cs[:, 1, 0:Tn], ALU.mult), reads=[bankT[bb], cs_t], writes=[b_t])
        P.op("pool", TT(sv[:, 0:Tn], av[:, 0:Tn], bv[:, 0:Tn], ALU.add), reads=[a_t, b_t], writes=[s_t])
        P.dma("pool", dst_scr[m][:, c0:c0 + Tn], sv[:, 0:Tn], s_t, reads=[s_t])

    def plain_proj(col0, dst_scr, hp, Tn, c0, hv, h_t, scale):
        ba = next_bank()
        for c in range(8):
            P.op("pe", MM(psF[:, ba, 0:Tn], Winv[:, c, col0 + hp * 128: col0 + (hp + 1) * 128], hv[:, c, 0:Tn], c == 0, c == 7),
                 reads=[Win_tk[(col0 + hp * 128) // 1024], h_t], writes=[bankT[ba]])
        k = kk["k"]
        kk["k"] += 1
        sv, s_t = kst[k % 4]
        P.op("act", AMUL(sv[:, 0:Tn], psF[:, ba, 0:Tn], scale), reads=[bankT[ba]], writes=[s_t])
        P.dma("pool", dst_scr[2 * hp][0:64, c0:c0 + Tn], sv[0:64, 0:Tn], s_t, reads=[s_t])
        P.dma("pool", dst_scr[2 * hp + 1][0:64, c0:c0 + Tn], sv[64:128, 0:Tn], s_t, reads=[s_t])

    def a2_loads(gi):
        tok0, nb, own = groups[gi]
        Tn = nb * 128
        hv, h_t = hnTb[gi % 2]
        P.dma("sp", hv[:, :, 0:Tn], hnT_scr[:, :, tok0:tok0 + Tn], h_t, writes=[h_t])
        cv, c_t = csK[gi % 2]
        P.dma("sp", cv[:, 0, 0:Tn], cosK[:, tok0:tok0 + Tn], c_t, writes=[c_t])
        P.dma("sp", cv[:, 1, 0:Tn], sinK[:, tok0:tok0 + Tn], c_t, writes=[c_t])
        if own is not None:
            qv, q_t = csQ[own % 2]
            P.dma("sp", qv[:, 0, :], cosQ[:, own * 512:(own + 1) * 512], q_t, writes=[q_t])
            P.dma("sp", qv[:, 1, :], sinQ[:, own * 512:(own + 1) * 512], q_t, writes=[q_t])

    def a2_vblock(tok0, r, hv, h_t):
        blk = (tok0 + r * 128) // 128
        k = kk["v"]
        kk["v"] += 1
        lhs = [hv[:, c, r * 128:(r + 1) * 128] for c in range(8)]
        ba = next_bank()
        for c in range(8):
            P.op("pe", MM(psF[:, ba, :], lhs[c], Winv[:, c, 1024:1536], c == 0, c == 7), reads=[Win_tk[1], h_t], writes=[bankT[ba]])
        sv, s_t = vd0 if blk == 0 else vdst[k % 2]
        P.op("act", ACTF(sv[:, :, 0:128], psF[:, ba, :].rearrange("p (h n) -> p h n", h=4), AF.Copy),
             reads=[bankT[ba]], writes=[s_t])
        P.dma("pool", Vd_scr[blk * 128:(blk + 1) * 128], sv[:, :, :], s_t, reads=[s_t])
        bb = next_bank()
        for c in range(8):
            P.op("pe", MM(psF[:, bb, :], lhs[c], Winv[:, c, 2560:3072], c == 0, c == 7), reads=[Win_tk[2], h_t], writes=[bankT[bb]])
        fv, f_t = vf0 if blk == 0 else vfst[k % 2]
        P.op("dve", CP(fv[:, :, 0:64], psF[:, bb, :].rearrange("p (h n) -> p h n", h=8)), reads=[bankT[bb]], writes=[f_t])
        P.dma("pool", Vf_scr[blk * 128:(blk + 1) * 128], fv[:, :, :], f_t, reads=[f_t])
        bc = next_bank()
        for c in range(8):
            P.op("pe", MM(psF[:, bc, 0:8], lhs[c], Winv[:, c, 3072:3080], c == 0, c == 7), reads=[Win_tk[3], h_t], writes=[bankT[bc]])
        zv, z_t = zt[k % 2]
        ev, e_t = et[k % 2]
        P.op("dve", TT(zv[:, :], psF[:, bc, 0:8], bfg[:, :], ALU.add), reads=[bankT[bc], bfg_t], writes=[z_t])
        P.op("act", ACTF(ev[:, :], zv[:, :], AF.Exp, scale=-1.0), reads=[z_t], writes=[e_t])
        P.op("act", ACTF(lpos[:, blk, :], ev[:, :], AF.Ln, bias=1.0), reads=[e_t], writes=[lpos_t])

    def a2_group(gi):
        tok0, nb, own = groups[gi]
        Tn = nb * 128
        if gi + 1 < len(groups):
            a2_loads(gi + 1)
        hv, h_t = hnTb[gi % 2]
        cv, c_t = csK[gi % 2]
        for m in range(4):
            rope_proj(512, cv, c_t, KTd_scr, m, Tn, tok0, hv, h_t)
        for hp in range(4):
            plain_proj(2048, KTf_scr, hp, Tn, tok0, hv, h_t, 1.0)
        if own is not None:
            qv, q_t = csQ[own % 2]
            for m in range(4):
                rope_proj(0, qv, q_t, QTd_scr, m, Tn, own * 512, hv, h_t)
            for hp in range(4):
                plain_proj(1536, QTf_scr, hp, Tn, own * 512, hv, h_t, 0.125)
        for r in range(nb):
            a2_vblock(tok0, r, hv, h_t)

    a2_loads(0)
    for gi in range(len(groups)):
        a2_group(gi)
    P.op("dve", MS(lpos[0:NPAD, 0, :], 0.0), writes=[lpos_t])
    end_phase()

    cum, cum_t = arena.tile([128, NBLK, 8], F32, "cum")
    totB, tot_t = arena.tile([128, NBLK, 8], F32, "totB")
    offB, off_t = arena.tile([128, NBLK, 8], F32, "offB")
    sbtot, sbt_t = arena.tile([128, 16, 8], F32, "sbtot")
    ptot, pt_t = arena.tile([128, 8, 8], F32, "ptot")
    Ppre, pp_t = arena.tile([128, 9, 8], F32, "Ppre")
    offsb, osb_t = arena.tile([128, 16, 8], F32, "offsb")
    biasS = [arena.tile([128, 8 * i + 9, 8], F32, "biasS%d" % i) for i in range(NSLOT)]
    neglam, nl_t = arena.tile([128, 1], F32, "neglam")
    subg, subg_t = arena.tile([128, 1], F32, "subg")
    b_mark = arena.off

    lflat = lpos.rearrange("p a b -> p (a b)")
    for (lhs, dstv, dst_t) in ((triF, cum, cum_t), (onesF, totB, tot_t)):
        b0, b1 = next_bank(), next_bank()
        P.op("pe", MM(psF[:, b0, 0:512], lhs, lflat[:, 0:512], True, True), reads=[cf_t, lpos_t], writes=[bankT[b0]])
        P.op("pe", MM(psF[:, b1, 0:8], lhs, lflat[:, 512:520], True, True), reads=[cf_t, lpos_t], writes=[bankT[b1]])
        dflat = dstv.rearrange("p a b -> p (a b)")
        P.op("dve", CP(dflat[:, 0:512], psF[:, b0, 0:512]), reads=[bankT[b0]], writes=[dst_t])
        P.op("dve", CP(dflat[:, 512:520], psF[:, b1, 0:8]), reads=[bankT[b1]], writes=[dst_t])
    tv = totB[:, 1:65, :].rearrange("p (j r) h -> p j r h", r=4)
    ov = offB[:, 1:65, :].rearrange("p (j r) h -> p j r h", r=4)
    P.op("dve", TT(sbtot[:, :, :], tv[:, :, 0, :], tv[:, :, 1, :], ALU.add), reads=[tot_t], writes=[sbt_t])
    P.op("dve", TT(sbtot[:, :, :], sbtot[:, :, :], tv[:, :, 2, :], ALU.add), reads=[tot_t, sbt_t], writes=[sbt_t])
    P.op("dve", TT(sbtot[:, :, :], sbtot[:, :, :], tv[:, :, 3, :], ALU.add), reads=[tot_t, sbt_t], writes=[sbt_t])
    sbv = sbtot.rearrange("p (i two) h -> p i two h", two=2)
    P.op("dve", TT(ptot[:, :, :], sbv[:, :, 0, :], sbv[:, :, 1, :], ALU.add), reads=[sbt_t], writes=[pt_t])
    P.op("dve", CP(Ppre[:, 0, :], totB[:, 0, :]), reads=[tot_t], writes=[pp_t])
    for i in range(8):
        P.op("dve", TT(Ppre[:, i + 1, :], Ppre[:, i, :], ptot[:, i, :], ALU.add), reads=[pp_t, pt_t], writes=[pp_t])
    osv = offsb.rearrange("p (i two) h -> p i two h", two=2)
    P.op("dve", STT(osv[:, :, 0, :], sbv[:, :, 1, :], flg[:, 1:2], Ppre[:, 0:8, :], ALU.mult, ALU.add),
         reads=[sbt_t, pp_t, flg_t], writes=[osb_t])
    P.op("dve", STT(osv[:, :, 1, :], sbv[:, :, 0, :], flg[:, 2:3], Ppre[:, 0:8, :], ALU.mult, ALU.add),
         reads=[sbt_t, pp_t, flg_t, osb_t], writes=[osb_t])
    P.op("dve", MS(offB[:, 0, :], 0.0), writes=[off_t])
    P.op("dve", CP(ov[:, :, 0, :], offsb[:, :, :]), reads=[osb_t, off_t], writes=[off_t])
    for r in range(1, 4):
        P.op("dve", TT(ov[:, :, r, :], ov[:, :, r - 1, :], tv[:, :, r - 1, :], ALU.add), reads=[off_t, tot_t], writes=[off_t])
    P.op("dve", TT(cum[:, :, :], cum[:, :, :], offB[:, :, :], ALU.add), reads=[cum_t, off_t], writes=[cum_t])
    for i in range(NSLOT):
        bv, b_t = biasS[i]
        nk = 8 * i + 9
        cin = offsb[:, 2 * i:2 * i + 1, :].broadcast_to([128, nk, 8])
        P.op("dve", TT(bv[:, :, :], cum[:, 0:nk, :], cin, ALU.subtract), reads=[cum_t, osb_t], writes=[b_t])
        P.op("dve", TS(bv[:, 8 * i + 5:8 * i + 9, :], bv[:, 8 * i + 5:8 * i + 9, :], flg[:, 0:1], ALU.add),
             reads=[b_t, flg_t], writes=[b_t])
    dsh = [arena.tile([128, 4, 8], F32, "dsh%d" % i) for i in range(2)]
    shs = [arena.tile([8, 512], BF16, "shs%d" % i) for i in range(2)]
    ones8, ones8_t = arena.tile([8, LP], BF16, "ones8")
    P.op("dve", MS(ones8, 1.0), writes=[ones8_t])
    P.dma("pool", KTf_scr[:, 64, :], ones8, ones8_t, reads=[ones8_t])
    for i in range(NSLOT):
        dv_, d_t = dsh[i % 2]
        b0 = 1 + 8 * i
        cin = offsb[:, 2 * i:2 * i + 1, :].broadcast_to([128, 4, 8])
        P.op("dve", TT(dv_[:, :, :], cin, cum[:, b0:b0 + 4, :], ALU.subtract), reads=[cum_t, osb_t], writes=[d_t])
        ba = next_bank()
        for r in range(4):
            P.op("pe", MM(psF[0:8, ba, r * 128:(r + 1) * 128], dv_[:, r, :], identF, True, True),
                 reads=[d_t, cf_t], writes=[bankT[ba]])
        sv, s_t = shs[i % 2]
        P.op("act", ACTF(sv[:, :], psF[0:8, ba, :], AF.Copy), reads=[bankT[ba]], writes=[s_t])
        P.dma("pool", QTf_scr[:, 64, i * 512:(i + 1) * 512], sv[:, :], s_t, reads=[s_t])
    lamt, lam_t = arena.tile([128, 4, 64], F32, "lamt")
    lamp, lamp_t = arena.tile([128, 2, 64], F32, "lamp")
    lams, lams_t = arena.tile([128, 4], F32, "lams")
    P.dma("sp", lamt.rearrange("p a b -> p (a b)"), lam_d.rearrange("a b -> (a b)").partition_broadcast(128), lam_t, writes=[lam_t])
    P.dma("sp", subg, subg_d.rearrange("(p o) -> p o", o=1), subg_t, writes=[subg_t])
    P.op("dve", MS(lams[:, :], 0.0), writes=[lams_t])
    for j in range(2):
        P.op("dve", STT(lamp[:, j, :], lamt[:, 2 * j, :], 1.0, lamt[:, 2 * j + 1, :], ALU.mult, ALU.mult, accum=lams[:, j:j + 1]),
             reads=[lam_t, lams_t], writes=[lamp_t, lams_t])
    P.op("act", ACTF(lams[:, 2:4], lams[:, 0:2], AF.Exp), reads=[lams_t], writes=[lams_t])
    P.op("dve", STT(neglam[:, :], lams[:, 3:4], -LAMBDA_INIT, lams[:, 2:3], ALU.add, ALU.subtract), reads=[lams_t], writes=[nl_t])
    P.op("dve", TS(subg[:, :], subg[:, :], (1.0 - LAMBDA_INIT), ALU.mult), reads=[subg_t], writes=[subg_t])
    P.barrier()
    arena.off = b_mark

    ubuf = []
    for i in range(2):
        ubuf.append(dict(KT=arena.tile([128, LP], BF16, "KT%d" % i), VA=arena.tile([128, NBLK, 130], BF16, "VA%d" % i),
                         QT=arena.tile([128, 2, 4096], BF16, "QT%d" % i)))
    for ub in ubuf:
        qv_, q_t_ = ub["QT"]
        P.op("dve", MS(qv_[64:128, 0, :], 0.0), writes=[q_t_])
        P.op("dve", MS(qv_[0:64, 1, :], 0.0), writes=[q_t_])
    Pt = [arena.tile([128, 512], BF16, "Pt%d" % i) for i in range(4)]
    onesb, onesb_t = arena.tile([128, 2, 128], BF16, "onesb")
    P.op("dve", MS(onesb, 1.0), writes=[onesb_t])
    P.op("dve", MS(onesb[0:NPAD, 1, :], 0.0), writes=[onesb_t])
    o1n, o1n_t = arena.tile([128, 512], F32, "o1n")
    uu, uu_t = arena.tile([128, 512], F32, "uu")
    usq, usq_t = arena.tile([128, 512], F32, "usq")
    rdb, rdb_t = arena.tile([128, 512], F32, "rdb")
    lnb, lnb_t = arena.tile([128, 512], F32, "lnb")
    rsb, rsb_t = arena.tile([128, 512], F32, "rsb")
    oTs = [arena.tile([128, 512], BF16, "oTs%d" % i) for i in range(2)]
    recf, recf_t = arena.tile([128, 512], F32, "recf")
    bcs, bcs_t = arena.tile([64, 512], F32, "bcs")
    accT = [T("acc0"), T("acc1")]
    psBf = psB[:, :].bitcast(F32)

    units = [("d", m) for m in range(4)] + [("f", h) for h in range(8)]

    def unit_load(ui):
        kind, idx = units[ui]
        ub = ubuf[ui % 2]
        (ktv, kt_t), (vav, va_t), (qtv, qt_t) = ub["KT"], ub["VA"], ub["QT"]
        if kind == "d":
            P.dma("sp", ktv[:, :], KTd_scr[idx], kt_t, writes=[kt_t])
            P.dma("sp", qtv[0:64, 0, :], QTd_scr[idx][0:64, :], qt_t, writes=[qt_t])
            P.dma("sp", qtv[64:128, 1, :], QTd_scr[idx][64:128, :], qt_t, writes=[qt_t])
            src = Vd_scr[:, idx, :].rearrange("(b k) n -> k b n", k=128)
            P.dma("sp", vav[:, :, :], src, va_t, writes=[va_t])
        else:
            P.dma("sp", ktv[0:65, :], KTf_scr[idx], kt_t, writes=[kt_t])
            P.dma("sp", qtv[0:65, 0, :], QTf_scr[idx], qt_t, writes=[qt_t])
            src = Vf_scr[:, idx, :].rearrange("(b k) n -> k b n", k=128)
            P.dma("sp", vav[:, :, 0:66], src, va_t, writes=[va_t])

    cnt = {"t": 0, "acc": 0, "ev": 0}

    def run_unit(ui):
        kind, idx = units[ui]
        ub = ubuf[ui % 2]
        (ktv, kt_t), (vav, va_t), (qtv, qt_t) = ub["KT"], ub["VA"], ub["QT"]
        dv = 128 if kind == "d" else 64
        maps = [0, 1] if kind == "d" else [0]
        tiles = []
        for i in range(NSLOT):
            for mp in maps:
                nk = 8 * i + 9
                for kb in range(nk):
                    diag = (8 * i + 1 <= kb <= 8 * i + 4)
                    other = kb > 8 * i + 4
                    rel = kb - (8 * i + 1) if diag else 0
                    tiles.append(dict(i=i, mp=mp, kb=kb, rel=rel, diag=diag, other=other, first=(kb == 0), last=(kb == nk - 1)))
        ntl = len(tiles)
        state = {}

        def emit_qk(tl):
            k = cnt["t"]
            cnt["t"] += 1
            bank = k % 3
            tl["pb"] = k % 4
            i, mp, kb = tl["i"], tl["mp"], tl["kb"]
            c0 = tl["rel"] * 128
            rows = slice(0, 128) if kind == "d" else slice(0, 65)
            P.op("pe", MM(psF[:, bank, c0:512], ktv[rows, kb * 128:(kb + 1) * 128], qtv[rows, mp, i * 512 + c0:(i + 1) * 512], True, True),
                 reads=[kt_t, qt_t], writes=[bankT[bank]])
            pv, p_t = Pt[tl["pb"]]
            if kind == "f":
                bias = biasS[i][0][:, kb, idx:idx + 1]
                rd = [bankT[bank], biasS[i][1]]
            elif tl["other"]:
                bias = flg[:, 0:1]
                rd = [bankT[bank], flg_t]
            else:
                bias = 0.0
                rd = [bankT[bank]]
            P.op("act", ACTF(pv[:, c0:512], psF[:, bank, c0:512], AF.Exp, bias=bias, scale=1.0), reads=rd, writes=[p_t])
            if tl["diag"]:
                sub = pv[:, c0:c0 + 128]
                P.op("dve", STT(sub, sub, 3.0e38, tri, ALU.min, ALU.mult), reads=[p_t, cb_t], writes=[p_t])

        deferred = []

        def zero_acc(aset):
            if kind == "d":
                P.op("dve", MS(psF[:, 3 + 2 * aset, :], 0.0), writes=[accT[aset]])
                P.op("dve", MS(psF[:, 4 + 2 * aset, :], 0.0), writes=[accT[aset]])
            else:
                P.op("dve", MS(psF[0:65, 3 + aset, :], 0.0), writes=[accT[aset]])

        def emit_av(tl, step):
            i, mp, kb = tl["i"], tl["mp"], tl["kb"]
            if tl["first"]:
                if (i, mp) not in state:
                    state[(i, mp)] = cnt["acc"] % 2
                    cnt["acc"] += 1
                    zero_acc(state[(i, mp)])
            aset = state[(i, mp)]
            pv, p_t = Pt[tl["pb"]]
            r0 = tl["rel"] if tl["diag"] else 0
            c0 = r0 * 128
            if kind == "d":
                P.op("pe", MM(psF[:, 3 + 2 * aset, c0:512], vav[:, kb, 0:128], pv[:, c0:512], False, False, skip=True),
                     reads=[p_t, va_t, accT[aset]], writes=[accT[aset]])
                P.op("pe", MM(psF[:, 4 + 2 * aset, c0:512], onesb[:, 1 if kb == 0 else 0, :], pv[:, c0:512], False, False, skip=True),
                     reads=[p_t, onesb_t, accT[aset]], writes=[accT[aset]])
            else:
                P.op("pe", MM(psF[0:65, 3 + aset, c0:512], vav[:, kb, 0:65], pv[:, c0:512], False, False, skip=True),
                     reads=[p_t, va_t, accT[aset]], writes=[accT[aset]])
            if tl["last"]:
                nxt = tl.get("next")
                if nxt is not None:
                    state[nxt] = cnt["acc"] % 2
                    cnt["acc"] += 1
                    zero_acc(state[nxt])
                if kind == "d":
                    emit_evac(i, mp, aset, step)
                else:
                    emit_evac_fox(i, aset, step)

        def emit_evac_fox(i, aset, step):
            k = cnt["ev"]
            cnt["ev"] += 1
            ov_, o_t = oTs[k % 2]
            P.op("dve", RCP(recf[64:65, :], psF[64:65, 3 + aset, :]), reads=[accT[aset]], writes=[recf_t])

            def stage_b():
                P.op("pe", MM(psF[0:64, 5, :], onesF[64:65, 0:64], recf[64:65, :], True, True), reads=[recf_t, cf_t], writes=[bankT[5]])

            def stage_c():
                P.op("dve", CP(bcs[:, :], psF[0:64, 5, :]), reads=[bankT[5]], writes=[bcs_t])
                P.op("dve", TT(ov_[0:64, :], psF[0:64, 3 + aset, :], bcs[:, :], ALU.mult), reads=[accT[aset], bcs_t], writes=[o_t])
                ch = 4 + idx // 2
                p0 = (idx % 2) * 64
                P.dma("pool", oT_scr[ch][p0:p0 + 64, i * 512:(i + 1) * 512], ov_[0:64, :], o_t, reads=[o_t])

            deferred.append((step + 2, stage_b))
            deferred.append((step + 4, stage_c))

        def emit_evac(i, mp, aset, step):
            k = cnt["ev"]
            cnt["ev"] += 1
            acc = psF[:, 3 + 2 * aset, :]
            den = psF[:, 4 + 2 * aset, :]
            P.op("dve", RCP(rdb[:, :], den), reads=[accT[aset]], writes=[rdb_t])
            if mp == 0:
                P.op("dve", TT(o1n[:, :], acc, rdb[:, :], ALU.mult), reads=[accT[aset], rdb_t], writes=[o1n_t])
                return
            P.op("dve", TT(uu[:, :], acc, rdb[:, :], ALU.mult), reads=[accT[aset], rdb_t], writes=[uu_t])
            P.op("dve", STT(uu[:, :], uu[:, :], neglam[:, 0:1], o1n[:, :], ALU.mult, ALU.add), reads=[uu_t, nl_t, o1n_t], writes=[uu_t])
            P.op("dve", TT(usq[:, :], uu[:, :], uu[:, :], ALU.mult), reads=[uu_t], writes=[usq_t])
            ov_, o_t = oTs[k % 2]

            def stage_b():
                P.op("pe", MM(psBf, onesF, usq[:, :], True, True), reads=[usq_t, cf_t], writes=[psBT])
                P.op("act", ACTF(lnb[:, :], psBf, AF.Ln, bias=1e-5, scale=1.0 / 128), reads=[psBT], writes=[lnb_t])
                P.op("act", ACTF(rsb[:, :], lnb[:, :], AF.Exp, scale=-0.5), reads=[lnb_t], writes=[rsb_t])

            def stage_c():
                P.op("dve", STT(ov_[:, :], uu[:, :], subg[:, 0:1], rsb[:, :], ALU.mult, ALU.mult), reads=[uu_t, subg_t, rsb_t], writes=[o_t])
                P.dma("pool", oT_scr[idx][:, i * 512:(i + 1) * 512], ov_[:, :], o_t, reads=[o_t])

            deferred.append((step + 3, stage_b))
            deferred.append((step + 6, stage_c))

        for t in range(ntl):
            if tiles[t]["last"] and t + 1 < ntl:
                tiles[t]["next"] = (tiles[t + 1]["i"], tiles[t + 1]["mp"])
        LAG = 2
        for t in range(ntl + LAG):
            if t < ntl:
                emit_qk(tiles[t])
            if t >= LAG:
                emit_av(tiles[t - LAG], t)
            while deferred and deferred[0][0] <= t:
                deferred.pop(0)[1]()
        while deferred:
            deferred.pop(0)[1]()

    unit_load(0)
    for ui in range(len(units)):
        if ui + 1 < len(units):
            unit_load(ui + 1)
        if ui == 4:
            P.barrier()
        run_unit(ui)
    end_phase()

    Wo = arena.tile([128, 8, D], BF16, "Wo")
    stg = [arena.tile([128, 1024], F32, "stgc%d" % i) for i in range(3)]
    Wo_tk = load_weights(w_out, D, D, Wo, None, stg)
    Wov, Wo_t = Wo
    oTb = [arena.tile([128, 8, 512], BF16, "oTb%d" % i) for i in range(2)]
    hb = [arena.tile([128, D], F32, "hb%d" % i) for i in range(6)]

    def c1_loads(i):
        ov_, o_t = oTb[i % 2]
        P.dma("sp", ov_[:, :, :], oT_scr[:, :, i * 512:(i + 1) * 512].rearrange("c p q -> p c q"), o_t, writes=[o_t])

    def c1_block(i, r, hk):
        ov_, o_t = oTb[i % 2]
        bv, b_t = hb[hk % 6]
        rows = slice(i * 512 + r * 128, i * 512 + (r + 1) * 128)
        P.dma("sp", bv[:, :], h1_scr[rows, :], b_t, writes=[b_t])
        for half in range(2):
            bank = next_bank()
            for c in range(8):
                P.op("pe", MM(psF[:, bank, :], ov_[:, c, r * 128:(r + 1) * 128], Wov[:, c, half * 512:(half + 1) * 512], c == 0, c == 7),
                     reads=[o_t, Wo_tk[0]], writes=[bankT[bank]])
            o = bv[:, half * 512:(half + 1) * 512]
            P.op("dve", TT(o, psF[:, bank, :], o, ALU.add), reads=[bankT[bank], b_t], writes=[b_t])
        P.dma("pool", h1_scr[rows, :], bv[:, :], b_t, reads=[b_t])

    c1_loads(0)
    hk = 0
    for i in range(NSLOT):
        if i + 1 < NSLOT:
            c1_loads(i + 1)
        for r in range(4):
            c1_block(i, r, hk)
            hk += 1
    end_phase()

    own_groups = [(i * 512, 4, i) for i in range(NSLOT)]

    def c2_extra():
        ex = dict(gf=arena.tile([128, D], F32, "gf"), ss3=arena.tile([128, 4], F32, "ss3"),
                  ln3=arena.tile([128, 4], F32, "ln3"), rs3=arena.tile([128, 4], F32, "rs3"), k=[0])
        gv, g_t = ex["gf"]
        P.dma("sp", gv[:, :], gfin.partition_broadcast(128), g_t, writes=[g_t])
        return ex

    def c2_load(g, r, buf):
        tok0, nb, own = g
        bv, b_t = buf
        P.dma("sp", bv[:, :], h1_scr[tok0 + r * 128: tok0 + (r + 1) * 128, :], b_t, writes=[b_t])

    def c2_post(g, r, buf, ex):
        tok0, nb, own = g
        bv, b_t = buf
        k = ex["k"][0]
        ex["k"][0] += 1
        col = k % 4
        ssv, ss_t = ex["ss3"]
        jv, j_t = ex["junk"]
        lv, l_t = ex["ln3"]
        rv, r_t = ex["rs3"]
        gv, g_t = ex["gf"]
        sumsq(bv[:, :], b_t, jv[:, :], j_t, ssv[:, col:col + 1], ss_t)
        rstd_chain(ssv[:, col:col + 1], ss_t, lv[:, col:col + 1], l_t, rv[:, col:col + 1], r_t, 1.0 / D, 1e-6)
        P.op("dve", STT(bv[:, :], bv[:, :], rv[:, col:col + 1], gv[:, :], ALU.mult, ALU.mult), reads=[b_t, r_t, g_t], writes=[b_t])
        P.dma("pool", out_d[tok0 + r * 128: tok0 + (r + 1) * 128, :], bv[:, :], b_t, reads=[b_t])
        return None

    ffn_phase(w_gu2, w_d2, gc[:, 16:24], own_groups, c2_load, c2_post, c2_extra)

    with nc.Block() as block:
        @block.tensor
        def _(e):
            for f in P.streams["pe"]:
                f(e)

        @block.scalar
        def _(e):
            for f in P.streams["act"]:
                f(e)

        @block.vector
        def _(e):
            for f in P.streams["dve"]:
                f(e)

        @block.gpsimd
        def _(e):
            for f in P.streams["pool"]:
                f(e)

        @block.sync
        def _(e):
            for f in P.streams["sp"]:
                f(e)
    return nc, P


def _host_constants():
    ident = np.eye(128, dtype=np.float32)
    tri = np.triu(np.ones((128, 128), dtype=np.float32))
    cbf = np.concatenate([ident, tri], axis=1).astype(ml_dtypes.bfloat16)
    cf32 = np.concatenate([tri, np.ones((128, 128), np.float32), ident], axis=1).astype(np.float32)
    return cbf, cf32


def _rope_tables(pos, qscale):
    inv_freq = np.power(np.float32(500000.0), -np.arange(0, 16, 2, dtype=np.float32) / np.float32(16)).astype(np.float32)
    ang = (pos.astype(np.float32)[None, :] * inv_freq[:, None]).astype(np.float32)
    c = np.cos(ang).astype(np.float32)
    s = np.sin(ang).astype(np.float32)
    n = pos.shape[0]
    cosT = np.ones((64, n), np.float32)
    sinT = np.zeros((64, n), np.float32)
    cosT[0:8] = c
    cosT[8:16] = c
    sinT[0:8] = -s
    sinT[8:16] = s
    cosT = np.tile(cosT, (2, 1)) * np.float32(qscale)
    sinT = np.tile(sinT, (2, 1)) * np.float32(qscale)
    return np.ascontiguousarray(cosT), np.ascontiguousarray(sinT)


_CACHE = {}


def kernel(x, meta_tokens, ffn1_norm_g, ffn1_w_gate_up, ffn1_w_down, mix_norm_g, w_in, b_forget,
           lam_q1, lam_k1, lam_q2, lam_k2, diff_subln_g, w_out, ffn2_norm_g, ffn2_w_gate_up,
           ffn2_w_down, final_norm_g):
    f32 = np.float32
    x = np.asarray(x, f32)
    B = x.shape[0]
    if "nc" not in _CACHE:
        _CACHE["nc"] = build_program()[0]
    nc = _CACHE["nc"]
    cbf, cf32 = _host_constants()
    w_in0 = np.ascontiguousarray(np.asarray(w_in, f32)[0])
    perm = np.arange(1024)
    for base in range(0, 1024, 64):
        perm[base:base + 8] = np.arange(base + 8, base + 16)
        perm[base + 8:base + 16] = np.arange(base, base + 8)
    w_sw = np.ascontiguousarray(w_in0[:, perm])
    gcols = np.concatenate([np.asarray(g, f32)[0].reshape(8, 128).T for g in (ffn1_norm_g, mix_norm_g, ffn2_norm_g)], axis=1)
    gcols = np.ascontiguousarray(gcols)
    lam = np.ascontiguousarray(np.stack([np.asarray(v, f32)[0] for v in (lam_q1, lam_k1, lam_q2, lam_k2)]))
    shared = dict(
        w_gu1=np.ascontiguousarray(np.asarray(ffn1_w_gate_up, f32)[0]), w_d1=np.ascontiguousarray(np.asarray(ffn1_w_down, f32)[0]),
        w_gu2=np.ascontiguousarray(np.asarray(ffn2_w_gate_up, f32)[0]), w_d2=np.ascontiguousarray(np.asarray(ffn2_w_down, f32)[0]),
        w_in=w_in0, w_sw=w_sw, w_out=np.ascontiguousarray(np.asarray(w_out, f32)[0]), gcols=gcols,
        gfin=np.ascontiguousarray(np.asarray(final_norm_g, f32)), bfg=np.ascontiguousarray(np.asarray(b_forget, f32)[0]),
        lam=lam, subg=np.ascontiguousarray(np.asarray(diff_subln_g, f32)[0]), cbf=cbf, cf32=cf32)
    in_maps = []
    meta = np.asarray(meta_tokens, f32)
    for core in range(8):
        b, c = core // 2, core % 2
        xa = np.zeros((LP, D), f32)
        xa[NPAD:128] = meta
        pos = np.zeros(LP, f32)
        pos[0:128] = np.arange(128) - NPAD
        posq = np.zeros(4096, f32)
        for j in range(16):
            i = j // 2
            a = 2 * i + c if j % 2 == 0 else 2 * i + 1 - c
            xa[128 + 512 * j: 128 + 512 * (j + 1)] = x[b, 512 * a: 512 * (a + 1)]
            p = 128 + 512 * a + np.arange(512) - NPAD
            pos[128 + 512 * j: 128 + 512 * (j + 1)] = p
            if j % 2 == 0:
                posq[512 * i: 512 * (i + 1)] = p
        cK, sK = _rope_tables(pos, 1.0)
        cQ, sQ = _rope_tables(posq, 0.125)
        fl = np.zeros((128, 4), f32)
        fl[:, 0] = 0.0 if c == 1 else -30000.0
        fl[:, 1] = float(c)
        fl[:, 2] = float(1 - c)
        m = dict(shared)
        m.update(x_arr=xa, cosK=cK, sinK=sK, cosQ=cQ, sinQ=sQ, flags=fl)
        in_maps.append(m)
    res = run_bass_kernel_spmd(nc, in_maps, core_ids=list(range(8)))
    out = np.zeros((B, SEQ, D), f32)
    for core in range(8):
        b, c = core // 2, core % 2
        o = np.asarray(res.results[core]["out"])
        for i in range(NSLOT):
            a = 2 * i + c
            out[b, 512 * a: 512 * (a + 1)] = o[512 * i: 512 * (i + 1)]
    return out
```

```python
import math
import numpy as np
import ml_dtypes
import concourse.bass as bass
import concourse.mybir as mybir
from concourse.bass_utils import run_bass_kernel_spmd

F32 = mybir.dt.float32
BF16 = mybir.dt.bfloat16
AF = mybir.ActivationFunctionType
ALU = mybir.AluOpType

D = 1024
DFF = 2816
NF = DFF // 128
SEQ = 8192
NMETA = 16
NPAD = 112
LP = 8320
NBLK = 65
NSLOT = 8
INC = 3080
LAMBDA_INIT = 0.8 - 0.6 * math.exp(-0.3 * 0)
ENG = ["pe", "act", "dve", "pool", "sp"]


def MM(out, lhsT, rhs, start, stop, skip=False):
    if skip:
        return lambda e: e.matmul(out, lhsT=lhsT, rhs=rhs, start=start, stop=stop, skip_group_check=True)
    return lambda e: e.matmul(out, lhsT=lhsT, rhs=rhs, start=start, stop=stop)


def TR(out, in_, idt):
    return lambda e: e.transpose(out, in_, idt)


def ACTF(out, in_, func, bias=None, scale=None):
    kw = {}
    if bias is not None:
        kw["bias"] = bias
    if scale is not None:
        kw["scale"] = scale
    return lambda e: e.activation(out=out, in_=in_, func=func, **kw)


def AMUL(out, in_, m):
    return lambda e: e.mul(out, in_, m)


def TT(out, in0, in1, op):
    return lambda e: e.tensor_tensor(out=out, in0=in0, in1=in1, op=op)


def TS(out, in0, s1, op0):
    return lambda e: e.tensor_scalar(out=out, in0=in0, scalar1=s1, scalar2=None, op0=op0)


def STT(out, in0, scalar, in1, op0, op1, accum=None):
    if accum is not None:
        return lambda e: e.scalar_tensor_tensor(out=out, in0=in0, scalar=scalar, in1=in1, op0=op0, op1=op1, accum_out=accum)
    return lambda e: e.scalar_tensor_tensor(out=out, in0=in0, scalar=scalar, in1=in1, op0=op0, op1=op1)


def CP(out, in_):
    return lambda e: e.tensor_copy(out=out, in_=in_)


def MS(ap, val):
    return lambda e: e.memset(ap, val)


def RCP(out, in_):
    return lambda e: e.reciprocal(out=out, in_=in_)


class Sem:
    __slots__ = ("h", "count", "dma")

    def __init__(self, nc, name, dma=False):
        self.h = nc.alloc_semaphore(name)
        self.count = 0
        self.dma = dma


class T:
    __slots__ = ("w", "r", "dsem", "name")

    def __init__(self, name=""):
        self.w = None
        self.r = {}
        self.dsem = {}
        self.name = name


class Prog:
    def __init__(self, nc):
        self.nc = nc
        self.streams = {e: [] for e in ENG}
        self.esem = {e: Sem(nc, "e_" + e) for e in ENG}
        self.waited = {e: {} for e in ENG}
        self.dsems = []
        self.free_dsems = {"sp": [], "pool": []}
        self.ninst = 0

    def _deps(self, eng, reads, writes):
        deps = {}
        own = self.esem[eng]

        def add(sv, is_w):
            if sv is None:
                return
            s, v = sv
            if s is own and (eng == "pe" or not is_w):
                return
            if s.dma:
                v = s.count
            if deps.get(s, 0) < v:
                deps[s] = v

        for t in reads:
            add(t.w, True)
        for t in writes:
            add(t.w, True)
            for s, v in t.r.items():
                add((s, v), False)
        out = []
        w = self.waited[eng]
        for s, v in deps.items():
            if w.get(s, 0) >= v:
                continue
            w[s] = v
            out.append((s.h, v))
        return out

    def op(self, eng, fn, reads=(), writes=()):
        waits = self._deps(eng, reads, writes)
        sem = self.esem[eng]
        sem.count += 1
        val = sem.count
        h = sem.h

        def emit(e):
            for sh, v in waits:
                e.wait_ge(sh, v)
            fn(e).then_inc(h, 1)

        self.streams[eng].append(emit)
        self.ninst += 1
        for t in writes:
            t.w = (sem, val)
            t.r = {}
        for t in reads:
            t.r[sem] = val

    def get_dsem(self, t, q):
        if q not in t.dsem:
            if self.free_dsems[q]:
                t.dsem[q] = self.free_dsems[q].pop()
            else:
                s = Sem(self.nc, "d%s%d" % (q, len(self.dsems)), dma=True)
                self.dsems.append(s)
                t.dsem[q] = s
        return t.dsem[q]

    def dma(self, q, out_ap, in_ap, st, reads=(), writes=()):
        waits = self._deps(q, reads, writes)
        ds = self.get_dsem(st, q)
        ds.count += 16
        val = ds.count
        h = ds.h

        def emit(e):
            for sh, v in waits:
                e.wait_ge(sh, v)
            e.dma_start(out=out_ap, in_=in_ap).then_inc(h, 16)

        self.streams[q].append(emit)
        self.ninst += 1
        for t in writes:
            t.w = (ds, val)
            t.r = {}
        for t in reads:
            t.r[ds] = val

    def barrier(self, release=()):
        sems = [s for s in list(self.esem.values()) + self.dsems if s.count > 0]
        for e in ENG:
            waits = []
            w = self.waited[e]
            for s in sems:
                if s is self.esem[e]:
                    continue
                if w.get(s, 0) >= s.count:
                    continue
                w[s] = s.count
                waits.append((s.h, s.count))
            if waits:
                def emit(en, waits=waits):
                    for sh, v in waits:
                        en.wait_ge(sh, v)
                self.streams[e].append(emit)
        for t in release:
            for q, sm in t.dsem.items():
                self.free_dsems[q].append(sm)
            t.dsem = {}


class Arena:
    def __init__(self, nc, nbytes):
        self.n32 = nbytes // 4
        self.h = nc.alloc_sbuf_tensor("arena", [128, self.n32], F32)
        self.off = 0
        self.mark_ = 0
        self.tokens = []

    def tile(self, shape, dtype, name=""):
        free = 1
        for s in shape[1:]:
            free *= s
        nb = free * (4 if dtype == F32 else 2)
        nb = (nb + 31) // 32 * 32
        assert self.off + nb <= self.n32 * 4, ("arena overflow", name, self.off, nb, self.n32 * 4)
        v = self.h[:, self.off // 4:(self.off + nb) // 4]
        if dtype != F32:
            v = v.bitcast(dtype)
        v = v[:, 0:free]
        if len(shape) == 3:
            v = v.rearrange("p (a b) -> p a b", a=shape[1])
        elif len(shape) == 4:
            v = v.rearrange("p (a b c) -> p a b c", a=shape[1], b=shape[2])
        if shape[0] < 128:
            v = v[0:shape[0]]
        self.off += nb
        t = T(name)
        self.tokens.append(t)
        return v, t

    def mark(self):
        self.mark_ = self.off
        self.tokens = []

    def reset(self):
        self.off = self.mark_
        toks = self.tokens
        self.tokens = []
        return toks


def build_program():
    nc = bass.Bass("TRN2", target_bir_lowering=False)
    P = Prog(nc)

    def din(name, shape, dt=F32):
        return nc.dram_tensor(name, list(shape), dt, kind="ExternalInput").ap()

    def dscr(name, shape, dt):
        return nc.dram_tensor(name, list(shape), dt, kind="Internal").ap()

    x_arr = din("x_arr", [LP, D])
    w_gu1 = din("w_gu1", [D, 2 * DFF])
    w_d1 = din("w_d1", [DFF, D])
    w_gu2 = din("w_gu2", [D, 2 * DFF])
    w_d2 = din("w_d2", [DFF, D])
    w_in = din("w_in", [D, INC])
    w_sw = din("w_sw", [D, 1024])
    w_out = din("w_out", [D, D])
    gcols = din("gcols", [128, 24])
    gfin = din("gfin", [D])
    bfg_d = din("bfg", [8])
    lam_d = din("lam", [4, 64])
    subg_d = din("subg", [128])
    cosK = din("cosK", [128, LP])
    sinK = din("sinK", [128, LP])
    cosQ = din("cosQ", [128, 4096])
    sinQ = din("sinQ", [128, 4096])
    cbf = din("cbf", [128, 256], BF16)
    cf32 = din("cf32", [128, 384])
    flags = din("flags", [128, 4])
    out_d = nc.dram_tensor("out", [4096, D], F32, kind="ExternalOutput").ap()

    h1_scr = dscr("h1_scr", [4096, D], F32)
    hnT_scr = dscr("hnT_scr", [128, 8, LP], BF16)
    KTd_scr = dscr("KTd_scr", [4, 128, LP], BF16)
    KTf_scr = dscr("KTf_scr", [8, 65, LP], BF16)
    Vd_scr = dscr("Vd_scr", [LP, 4, 130], BF16)
    Vf_scr = dscr("Vf_scr", [LP, 8, 66], BF16)
    QTd_scr = dscr("QTd_scr", [4, 128, 4096], BF16)
    QTf_scr = dscr("QTf_scr", [8, 65, 4096], BF16)
    oT_scr = dscr("oT_scr", [8, 128, 4096], BF16)

    arena = Arena(nc, (nc.sbuf_bytes_remaining - 256) // 32 * 32)
    psF = nc.alloc_psum_tensor("psF", [128, 7, 512], F32)
    psB = nc.alloc_psum_tensor("psB", [128, 1024], BF16)
    bankT = [T("bank%d" % i) for i in range(7)]
    psBT = T("psB")
    psB3 = psB[:, :].rearrange("p (c t) -> p c t", c=8)

    cb, cb_t = arena.tile([128, 256], BF16, "cbf")
    ident = cb[:, 0:128]
    tri = cb[:, 128:256]
    cf, cf_t = arena.tile([128, 384], F32, "cf32")
    triF = cf[:, 0:128]
    onesF = cf[:, 128:256]
    identF = cf[:, 256:384]
    gc, gc_t = arena.tile([128, 24], F32, "gcols")
    flg, flg_t = arena.tile([128, 4], F32, "flags")
    bfg, bfg_t = arena.tile([128, 8], F32, "bfg")
    lpos, lpos_t = arena.tile([128, NBLK, 8], F32, "lpos")
    P.dma("sp", cb, cbf, cb_t, writes=[cb_t])
    P.dma("sp", cf, cf32, cf_t, writes=[cf_t])
    P.dma("sp", gc, gcols, gc_t, writes=[gc_t])
    P.dma("sp", flg, flags, flg_t, writes=[flg_t])
    P.dma("sp", bfg, bfg_d.partition_broadcast(128), bfg_t, writes=[bfg_t])
    base_mark = arena.off
    arena.tokens = []

    def end_phase(mark=None):
        arena.off = base_mark if mark is None else mark
        toks = arena.tokens
        arena.tokens = []
        P.barrier(release=toks)

    groups = [(0, 1, None)]
    for j in range(16):
        groups.append((128 + 512 * j, 4, (j // 2) if j % 2 == 0 else None))

    rr = {"cast": 0, "bank": 0}

    def next_bank():
        b = rr["bank"] % 7
        rr["bank"] += 1
        return b

    def weight_chunks(wd, K, N, dst, scale_cols, order=None):
        dstv, dst_t = dst
        nblk = (N + 1023) // 1024
        toks = [T("wblk%d" % i) for i in range(nblk)]
        chunks = []
        for bi in (order if order is not None else range(nblk)):
            n0 = bi * 1024
            w = min(1024, N - n0)
            for c in range(K // 128):
                def emit(stg, bi=bi, n0=n0, w=w, c=c):
                    sb, sb_t = stg[rr["cast"] % len(stg)]
                    P.dma("sp", sb[:, 0:w], wd[c * 128:(c + 1) * 128, n0:n0 + w], sb_t, writes=[sb_t])
                    o = dstv[:, c, n0:n0 + w]
                    i = sb[:, 0:w]
                    if scale_cols is not None:
                        sc = scale_cols[:, c:c + 1]
                        if rr["cast"] % 2 == 0:
                            P.op("act", AMUL(o, i, sc), reads=[sb_t, gc_t], writes=[toks[bi]])
                        else:
                            P.op("dve", TS(o, i, sc, ALU.mult), reads=[sb_t, gc_t], writes=[toks[bi]])
                    else:
                        if rr["cast"] % 2 == 0:
                            P.op("act", ACTF(o, i, AF.Copy), reads=[sb_t], writes=[toks[bi]])
                        else:
                            P.op("dve", CP(o, i), reads=[sb_t], writes=[toks[bi]])
                    rr["cast"] += 1
                chunks.append(emit)
        return toks, chunks

    def load_weights(wd, K, N, dst, scale_cols, stg, order=None):
        toks, chunks = weight_chunks(wd, K, N, dst, scale_cols, order)
        for ch in chunks:
            ch(stg)
        return toks

    def rstd_chain(ss_ap, ss_t, ln_ap, ln_t, rs_ap, rs_t, inv_d, eps):
        P.op("act", ACTF(ln_ap, ss_ap, AF.Ln, bias=eps, scale=inv_d), reads=[ss_t], writes=[ln_t])
        P.op("act", ACTF(rs_ap, ln_ap, AF.Exp, scale=-0.5), reads=[ln_t], writes=[rs_t])

    def sumsq(src_ap, src_t, junk_ap, junk_t, acc_ap_, acc_t):
        P.op("dve", MS(acc_ap_, 0.0), writes=[acc_t])
        P.op("dve", STT(junk_ap, src_ap, 1.0, src_ap, ALU.mult, ALU.mult, accum=acc_ap_),
             reads=[src_t, acc_t], writes=[junk_t, acc_t])

    def transpose8(src_ap, src_t, dst_ap, dst_t):
        for c in range(8):
            P.op("pe", TR(psB[:, c * 128:(c + 1) * 128], src_ap[:, c * 128:(c + 1) * 128], ident),
                 reads=[src_t, cb_t], writes=[psBT])
        P.op("dve", CP(dst_ap, psB3), reads=[psBT], writes=[dst_t])

    def ffn_phase(w_gu, w_d, gcol, grp_list, load_block, post_block, extra_alloc, preloaded=None):
        if preloaded is None:
            Wgu = arena.tile([128, 8, 2 * DFF], BF16, "Wgu")
            Wd = arena.tile([128, NF, D], BF16, "Wd")
        else:
            Wgu, Wd = preloaded[0], preloaded[1]
        xa = [arena.tile([128, D], F32, "xa%d" % i) for i in range(2)]
        xd = [arena.tile([128, D], F32, "xd%d" % i) for i in range(3)]
        xn = arena.tile([128, D], BF16, "xn")
        xnT = [arena.tile([128, 8, 512], BF16, "xnT%d" % i) for i in range(2)]
        hT = arena.tile([128, NF, 512], BF16, "hT")
        sg = arena.tile([128, 512], F32, "sg")
        ss = arena.tile([128, 4], F32, "ss")
        lnv = arena.tile([128, 4], F32, "lnv")
        rstd = arena.tile([128, 4], F32, "rstd")
        extra = extra_alloc()
        extra["junk"] = xn
        extra["hn"] = xn
        Wguv, Wgu_t = Wgu
        Wdv, Wd_t = Wd
        hTv, hT_t = hT
        xnv, xn_t = xn
        sgv, sg_t = sg
        ssv, ss_t = ss
        lnvv, lnv_t = lnv
        rsv, rs_t = rstd
        ctr = {"n": 0, "d": 0}
        ngrp = len(grp_list)
        nbuf = {}

        def stage_n_load(gi, r):
            k = ctr["n"]
            ctr["n"] += 1
            buf = xa[k % 2]
            load_block(grp_list[gi], r, buf)
            nbuf[(gi, r)] = (buf, k % 4)

        def stage_n_chain(gi, r):
            (bv, b_t), col = nbuf[(gi, r)]
            sumsq(bv[:, :], b_t, xnv[:, :], xn_t, ssv[:, col:col + 1], ss_t)
            rstd_chain(ssv[:, col:col + 1], ss_t, lnvv[:, col:col + 1], lnv_t, rsv[:, col:col + 1], rs_t, 1.0 / D, 1e-6)
            P.op("act", AMUL(xnv[:, :], bv[:, :], rsv[:, col:col + 1]), reads=[b_t, rs_t], writes=[xn_t])

        def stage_n_tr(gi, r):
            tv_, t_t = xnT[gi % 2]
            transpose8(xnv, xn_t, tv_[:, :, r * 128:(r + 1) * 128], t_t)

        def stage_gu(gi, f):
            tok0, nb, own = grp_list[gi]
            Tn = nb * 128
            tv_, t_t = xnT[gi % 2]
            pg = (rr["bank"] % 2) * 2
            rr["bank"] += 1
            for half, bank in ((0, pg), (1, pg + 1)):
                col = half * DFF + f * 128
                for c in range(8):
                    P.op("pe", MM(psF[:, bank, 0:Tn], Wguv[:, c, col:col + 128], tv_[:, c, 0:Tn], c == 0, c == 7),
                         reads=[Wgu_tk[col // 1024], t_t], writes=[bankT[bank]])
            P.op("act", ACTF(sgv[:, 0:Tn], psF[:, pg, 0:Tn], AF.Silu), reads=[bankT[pg]], writes=[sg_t])
            P.op("dve", TT(hTv[:, f, 0:Tn], sgv[:, 0:Tn], psF[:, pg + 1, 0:Tn], ALU.mult),
                 reads=[sg_t, bankT[pg + 1]], writes=[hT_t])

        dbuf = {}

        def d_load(gi, r):
            k = ctr["d"]
            ctr["d"] += 1
            buf = xd[k % 3]
            load_block(grp_list[gi], r, buf)
            dbuf[(gi, r)] = buf

        def stage_d(gi, r):
            bv, b_t = dbuf[(gi, r)]
            for half in range(2):
                bank = 4 + (rr["bank"] % 2)
                rr["bank"] += 1
                for f in range(NF):
                    P.op("pe", MM(psF[:, bank, :], hTv[:, f, r * 128:(r + 1) * 128], Wdv[:, f, half * 512:(half + 1) * 512],
                                  f == 0, f == NF - 1), reads=[hT_t, Wd_tk[0]], writes=[bankT[bank]])
                o = bv[:, half * 512:(half + 1) * 512]
                P.op("dve", STT(o, psF[:, bank, :], 0.5, o, ALU.mult, ALU.add), reads=[bankT[bank], b_t], writes=[b_t])

        for r in range(grp_list[0][1]):
            stage_n_load(0, r)
            stage_n_chain(0, r)
            stage_n_tr(0, r)
        if preloaded is None:
            Wgu_tk = load_weights(w_gu, D, 2 * DFF, Wgu, gcol, xd, order=[0, 2, 3, 1, 4, 5])
            Wd_tk = load_weights(w_d, DFF, D, Wd, None, xd)
        else:
            Wgu_tk, Wd_tk = preloaded[2], preloaded[3]
        pending = None
        for gi, g in enumerate(grp_list):
            tok0, nb, own = g
            nb_next = grp_list[gi + 1][1] if gi + 1 < ngrp else 0
            for f in range(NF):
                stage_gu(gi, f)
                if f == 1 and pending is not None:
                    pending()
                    pending = None
                r_, ph = divmod(f, 5)
                if r_ < nb_next:
                    if ph == 0:
                        stage_n_load(gi + 1, r_)
                    elif ph == 2:
                        stage_n_chain(gi + 1, r_)
                    elif ph == 4:
                        stage_n_tr(gi + 1, r_)
                if f == 16:
                    d_load(gi, 0)
                    if nb > 1:
                        d_load(gi, 1)
            for r in range(nb):
                stage_d(gi, r)
                if pending is not None:
                    pending()
                    pending = None
                pending = post_block(g, r, dbuf[(gi, r)], extra)
                if r + 2 < nb:
                    d_load(gi, r + 2)
        if pending is not None:
            pending()
        end_phase()

    def a1_extra():
        return dict(hnTs=arena.tile([128, 8, 128], BF16, "hnTs"),
                    ss2=arena.tile([128, 4], F32, "ss2"), ln2=arena.tile([128, 4], F32, "ln2"),
                    rs2=arena.tile([128, 4], F32, "rs2"), k=[0])

    def a1_load(g, r, buf):
        tok0, nb, own = g
        bv, b_t = buf
        P.dma("sp", bv[:, :], x_arr[tok0 + r * 128: tok0 + (r + 1) * 128, :], b_t, writes=[b_t])

    def a1_post(g, r, buf, ex):
        tok0, nb, own = g
        bv, b_t = buf
        if own is not None:
            P.dma("pool", h1_scr[own * 512 + r * 128: own * 512 + (r + 1) * 128, :], bv[:, :], b_t, reads=[b_t])
        k = ex["k"][0]
        ex["k"][0] += 1
        col = k % 4
        ssv, ss_t = ex["ss2"]
        lv, l_t = ex["ln2"]
        rv, r_t = ex["rs2"]
        hv, h_t = ex["hn"]
        sumsq(bv[:, :], b_t, hv[:, :], h_t, ssv[:, col:col + 1], ss_t)
        rstd_chain(ssv[:, col:col + 1], ss_t, lv[:, col:col + 1], l_t, rv[:, col:col + 1], r_t, 1.0 / D, 1e-6)
        P.op("act", AMUL(hv[:, :], bv[:, :], rv[:, col:col + 1]), reads=[b_t, r_t], writes=[h_t])

        def part2():
            sv, s_t = ex["hnTs"]
            transpose8(hv, h_t, sv[:, :, :], s_t)
            P.dma("pool", hnT_scr[:, :, tok0 + r * 128: tok0 + (r + 1) * 128], sv[:, :, :], s_t, reads=[s_t])
        return part2

    ffn_phase(w_gu1, w_d1, gc[:, 0:8], groups, a1_load, a1_post, a1_extra)

    Win = arena.tile([128, 8, INC], BF16, "Win")
    Wsw = arena.tile([128, 8, 1024], BF16, "Wsw")
    stg = [arena.tile([128, 1024], F32, "stg%d" % i) for i in range(3)]
    Winv, Win_t = Win
    Wswv, Wsw_t = Wsw
    hnTb = [arena.tile([128, 8, 512], BF16, "hnTb%d" % i) for i in range(2)]
    csK = [arena.tile([128, 2, 512], F32, "csK%d" % i) for i in range(2)]
    csQ = [arena.tile([128, 2, 512], F32, "csQ%d" % i) for i in range(2)]
    t1 = [arena.tile([128, 512], F32, "t1_%d" % i) for i in range(2)]
    t2 = [arena.tile([128, 512], F32, "t2_%d" % i) for i in range(2)]
    kst = [arena.tile([128, 512], BF16, "kst%d" % i) for i in range(4)]
    vdst = [arena.tile([128, 4, 130], BF16, "vdst%d" % i) for i in range(2)]
    vd0 = arena.tile([128, 4, 130], BF16, "vd0")
    vfst = [arena.tile([128, 8, 66], BF16, "vfst%d" % i) for i in range(2)]
    vf0 = arena.tile([128, 8, 66], BF16, "vf0")
    zt = [arena.tile([128, 8], F32, "zt%d" % i) for i in range(2)]
    et = [arena.tile([128, 8], F32, "et%d" % i) for i in range(2)]
    for (v, t) in vdst + vfst + [vd0, vf0]:
        P.op("dve", MS(v, 1.0), writes=[t])
    P.op("dve", MS(vd0[0][0:NPAD], 0.0), writes=[vd0[1]])
    P.op("dve", MS(vf0[0][0:NPAD], 0.0), writes=[vf0[1]])
    kk = {"k": 0, "v": 0}

    def rope_proj(col0, cs, cs_t, dst_scr, m, Tn, c0, hv, h_t):
        ba, bb = next_bank(), next_bank()
        for c in range(8):
            P.op("pe", MM(psF[:, ba, 0:Tn], Winv[:, c, col0 + m * 128: col0 + (m + 1) * 128], hv[:, c, 0:Tn], c == 0, c == 7),
                 reads=[Win_tk[(col0 + m * 128) // 1024], h_t], writes=[bankT[ba]])
        for c in range(8):
            P.op("pe", MM(psF[:, bb, 0:Tn], Wswv[:, c, col0 + m * 128: col0 + (m + 1) * 128], hv[:, c, 0:Tn], c == 0, c == 7),
                 reads=[Wsw_tk[0], h_t], writes=[bankT[bb]])
        k = kk["k"]
        kk["k"] += 1
        av, a_t = t1[k % 2]
        bv, b_t = t2[k % 2]
        sv, s_t = kst[k % 4]
        P.op("dve", TT(av[:, 0:Tn], psF[:, ba, 0:Tn], cs[:, 0, 0:Tn], ALU.mult), reads=[bankT[ba], cs_t], writes=[a_t])
        P.op("dve", TT(bv[:, 0:Tn], psF[:, bb, 0:Tn], cs[:, 1, 0:Tn], ALU.mult), reads=[bankT[bb], cs_t], writes=[b_t])
        P.op("pool", TT(sv[:, 0:Tn], av[:, 0:Tn], bv[:, 0:Tn], ALU.add), reads=[a_t, b_t], writes=[s_t])
        P.dma("pool", dst_scr[m][:, c0:c0 + Tn], sv[:, 0:Tn], s_t, reads=[s_t])

    def plain_proj(col0, dst_scr, hp, Tn, c0, hv, h_t, scale):
        ba = next_bank()
        for c in range(8):
            P.op("pe", MM(psF[:, ba, 0:Tn], Winv[:, c, col0 + hp * 128: col0 + (hp + 1) * 128], hv[:, c, 0:Tn], c == 0, c == 7),
                 reads=[Win_tk[(col0 + hp * 128) // 1024], h_t], writes=[bankT[ba]])
        k = kk["k"]
        kk["k"] += 1
        sv, s_t = kst[k % 4]
        P.op("act", AMUL(sv[:, 0:Tn], psF[:, ba, 0:Tn], scale), reads=[bankT[ba]], writes=[s_t])
        P.dma("pool", dst_scr[2 * hp][0:64, c0:c0 + Tn], sv[0:64, 0:Tn], s_t, reads=[s_t])
        P.dma("pool", dst_scr[2 * hp + 1][0:64, c0:c0 + Tn], sv[64:128, 0:Tn], s_t, reads=[s_t])

    def a2_loads(gi):
        tok0, nb, own = groups[gi]
        Tn = nb * 128
        hv, h_t = hnTb[gi % 2]
        P.dma("sp", hv[:, :, 0:Tn], hnT_scr[:, :, tok0:tok0 + Tn], h_t, writes=[h_t])
        cv, c_t = csK[gi % 2]
        P.dma("sp", cv[:, 0, 0:Tn], cosK[:, tok0:tok0 + Tn], c_t, writes=[c_t])
        P.dma("sp", cv[:, 1, 0:Tn], sinK[:, tok0:tok0 + Tn], c_t, writes=[c_t])
        if own is not None:
            qv, q_t = csQ[own % 2]
            P.dma("sp", qv[:, 0, :], cosQ[:, own * 512:(own + 1) * 512], q_t, writes=[q_t])
            P.dma("sp", qv[:, 1, :], sinQ[:, own * 512:(own + 1) * 512], q_t, writes=[q_t])

    def a2_vblock(tok0, r, hv, h_t):
        blk = (tok0 + r * 128) // 128
        k = kk["v"]
        kk["v"] += 1
        lhs = [hv[:, c, r * 128:(r + 1) * 128] for c in range(8)]
        ba = next_bank()
        for c in range(8):
            P.op("pe", MM(psF[:, ba, :], lhs[c], Winv[:, c, 1024:1536], c == 0, c == 7), reads=[Win_tk[1], h_t], writes=[bankT[ba]])
        sv, s_t = vd0 if blk == 0 else vdst[k % 2]
        P.op("act", ACTF(sv[:, :, 0:128], psF[:, ba, :].rearrange("p (h n) -> p h n", h=4), AF.Copy),
             reads=[bankT[ba]], writes=[s_t])
        P.dma("pool", Vd_scr[blk * 128:(blk + 1) * 128], sv[:, :, :], s_t, reads=[s_t])
        bb = next_bank()
        for c in range(8):
            P.op("pe", MM(psF[:, bb, :], lhs[c], Winv[:, c, 2560:3072], c == 0, c == 7), reads=[Win_tk[2], h_t], writes=[bankT[bb]])
        fv, f_t = vf0 if blk == 0 else vfst[k % 2]
        P.op("dve", CP(fv[:, :, 0:64], psF[:, bb, :].rearrange("p (h n) -> p h n", h=8)), reads=[bankT[bb]], writes=[f_t])
        P.dma("pool", Vf_scr[blk * 128:(blk + 1) * 128], fv[:, :, :], f_t, reads=[f_t])
        bc = next_bank()
        for c in range(8):
            P.op("pe", MM(psF[:, bc, 0:8], lhs[c], Winv[:, c, 3072:3080], c == 0, c == 7), reads=[Win_tk[3], h_t], writes=[bankT[bc]])
        zv, z_t = zt[k % 2]
        ev, e_t = et[k % 2]
        P.op("dve", TT(zv[:, :], psF[:, bc, 0:8], bfg[:, :], ALU.add), reads=[bankT[bc], bfg_t], writes=[z_t])
        P.op("act", ACTF(ev[:, :], zv[:, :], AF.Exp, scale=-1.0), reads=[z_t], writes=[e_t])
        P.op("act", ACTF(lpos[:, blk, :], ev[:, :], AF.Ln, bias=1.0), reads=[e_t], writes=[lpos_t])

    def a2_group(gi):
        tok0, nb, own = groups[gi]
        Tn = nb * 128
        if gi + 1 < len(groups):
            a2_loads(gi + 1)
        hv, h_t = hnTb[gi % 2]
        cv, c_t = csK[gi % 2]
        for m in range(4):
            rope_proj(512, cv, c_t, KTd_scr, m, Tn, tok0, hv, h_t)
        for hp in range(4):
            plain_proj(2048, KTf_scr, hp, Tn, tok0, hv, h_t, 1.0)
        if own is not None:
            qv, q_t = csQ[own % 2]
            for m in range(4):
                rope_proj(0, qv, q_t, QTd_scr, m, Tn, own * 512, hv, h_t)
            for hp in range(4):
                plain_proj(1536, QTf_scr, hp, Tn, own * 512, hv, h_t, 0.125)
        for r in range(nb):
            a2_vblock(tok0, r, hv, h_t)

    a2_loads(0)
    Win_tk = load_weights(w_in, D, INC, Win, gc[:, 8:16], stg, order=[0])
    Wsw_tk = load_weights(w_sw, D, 1024, Wsw, gc[:, 8:16], stg)
    Win_tk2 = load_weights(w_in, D, INC, Win, gc[:, 8:16], stg, order=[2, 1, 3])
    for _bi in (1, 2, 3):
        Win_tk[_bi] = Win_tk2[_bi]
    for gi in range(len(groups)):
        a2_group(gi)
    P.op("dve", MS(lpos[0:NPAD, 0, :], 0.0), writes=[lpos_t])
    end_phase()

    cum, cum_t = arena.tile([128, NBLK, 8], F32, "cum")
    totB, tot_t = arena.tile([128, NBLK, 8], F32, "totB")
    offB, off_t = arena.tile([128, NBLK, 8], F32, "offB")
    sbtot, sbt_t = arena.tile([128, 16, 8], F32, "sbtot")
    ptot, pt_t = arena.tile([128, 8, 8], F32, "ptot")
    Ppre, pp_t = arena.tile([128, 9, 8], F32, "Ppre")
    offsb, osb_t = arena.tile([128, 16, 8], F32, "offsb")
    biasS = [arena.tile([128, 8 * i + 9, 8], F32, "biasS%d" % i) for i in range(NSLOT)]
    neglam, nl_t = arena.tile([128, 1], F32, "neglam")
    subg, subg_t = arena.tile([128, 1], F32, "subg")
    b_mark = arena.off

    lflat = lpos.rearrange("p a b -> p (a b)")
    for (lhs, dstv, dst_t) in ((triF, cum, cum_t), (onesF, totB, tot_t)):
        b0, b1 = next_bank(), next_bank()
        P.op("pe", MM(psF[:, b0, 0:512], lhs, lflat[:, 0:512], True, True), reads=[cf_t, lpos_t], writes=[bankT[b0]])
        P.op("pe", MM(psF[:, b1, 0:8], lhs, lflat[:, 512:520], True, True), reads=[cf_t, lpos_t], writes=[bankT[b1]])
        dflat = dstv.rearrange("p a b -> p (a b)")
        P.op("dve", CP(dflat[:, 0:512], psF[:, b0, 0:512]), reads=[bankT[b0]], writes=[dst_t])
        P.op("dve", CP(dflat[:, 512:520], psF[:, b1, 0:8]), reads=[bankT[b1]], writes=[dst_t])
    tv = totB[:, 1:65, :].rearrange("p (j r) h -> p j r h", r=4)
    ov = offB[:, 1:65, :].rearrange("p (j r) h -> p j r h", r=4)
    P.op("dve", TT(sbtot[:, :, :], tv[:, :, 0, :], tv[:, :, 1, :], ALU.add), reads=[tot_t], writes=[sbt_t])
    P.op("dve", TT(sbtot[:, :, :], sbtot[:, :, :], tv[:, :, 2, :], ALU.add), reads=[tot_t, sbt_t], writes=[sbt_t])
    P.op("dve", TT(sbtot[:, :, :], sbtot[:, :, :], tv[:, :, 3, :], ALU.add), reads=[tot_t, sbt_t], writes=[sbt_t])
    sbv = sbtot.rearrange("p (i two) h -> p i two h", two=2)
    P.op("dve", TT(ptot[:, :, :], sbv[:, :, 0, :], sbv[:, :, 1, :], ALU.add), reads=[sbt_t], writes=[pt_t])
    P.op("dve", CP(Ppre[:, 0, :], totB[:, 0, :]), reads=[tot_t], writes=[pp_t])
    for i in range(8):
        P.op("dve", TT(Ppre[:, i + 1, :], Ppre[:, i, :], ptot[:, i, :], ALU.add), reads=[pp_t, pt_t], writes=[pp_t])
    osv = offsb.rearrange("p (i two) h -> p i two h", two=2)
    P.op("dve", STT(osv[:, :, 0, :], sbv[:, :, 1, :], flg[:, 1:2], Ppre[:, 0:8, :], ALU.mult, ALU.add),
         reads=[sbt_t, pp_t, flg_t], writes=[osb_t])
    P.op("dve", STT(osv[:, :, 1, :], sbv[:, :, 0, :], flg[:, 2:3], Ppre[:, 0:8, :], ALU.mult, ALU.add),
         reads=[sbt_t, pp_t, flg_t, osb_t], writes=[osb_t])
    P.op("dve", MS(offB[:, 0, :], 0.0), writes=[off_t])
    P.op("dve", CP(ov[:, :, 0, :], offsb[:, :, :]), reads=[osb_t, off_t], writes=[off_t])
    for r in range(1, 4):
        P.op("dve", TT(ov[:, :, r, :], ov[:, :, r - 1, :], tv[:, :, r - 1, :], ALU.add), reads=[off_t, tot_t], writes=[off_t])
    P.op("dve", TT(cum[:, :, :], cum[:, :, :], offB[:, :, :], ALU.add), reads=[cum_t, off_t], writes=[cum_t])
    for i in range(NSLOT):
        bv, b_t = biasS[i]
        nk = 8 * i + 9
        cin = offsb[:, 2 * i:2 * i + 1, :].broadcast_to([128, nk, 8])
        P.op("dve", TT(bv[:, :, :], cum[:, 0:nk, :], cin, ALU.subtract), reads=[cum_t, osb_t], writes=[b_t])
        P.op("dve", TS(bv[:, 8 * i + 5:8 * i + 9, :], bv[:, 8 * i + 5:8 * i + 9, :], flg[:, 0:1], ALU.add),
             reads=[b_t, flg_t], writes=[b_t])
    dsh = [arena.tile([128, 4, 8], F32, "dsh%d" % i) for i in range(2)]
    shs = [arena.tile([8, 512], BF16, "shs%d" % i) for i in range(2)]
    ones8, ones8_t = arena.tile([8, LP], BF16, "ones8")
    P.op("dve", MS(ones8, 1.0), writes=[ones8_t])
    P.dma("pool", KTf_scr[:, 64, :], ones8, ones8_t, reads=[ones8_t])
    for i in range(NSLOT):
        dv_, d_t = dsh[i % 2]
        b0 = 1 + 8 * i
        cin = offsb[:, 2 * i:2 * i + 1, :].broadcast_to([128, 4, 8])
        P.op("dve", TT(dv_[:, :, :], cin, cum[:, b0:b0 + 4, :], ALU.subtract), reads=[cum_t, osb_t], writes=[d_t])
        ba = next_bank()
        for r in range(4):
            P.op("pe", MM(psF[0:8, ba, r * 128:(r + 1) * 128], dv_[:, r, :], identF, True, True),
                 reads=[d_t, cf_t], writes=[bankT[ba]])
        sv, s_t = shs[i % 2]
        P.op("act", ACTF(sv[:, :], psF[0:8, ba, :], AF.Copy), reads=[bankT[ba]], writes=[s_t])
        P.dma("pool", QTf_scr[:, 64, i * 512:(i + 1) * 512], sv[:, :], s_t, reads=[s_t])
    lamt, lam_t = arena.tile([128, 4, 64], F32, "lamt")
    lamp, lamp_t = arena.tile([128, 2, 64], F32, "lamp")
    lams, lams_t = arena.tile([128, 4], F32, "lams")
    P.dma("sp", lamt.rearrange("p a b -> p (a b)"), lam_d.rearrange("a b -> (a b)").partition_broadcast(128), lam_t, writes=[lam_t])
    P.dma("sp", subg, subg_d.rearrange("(p o) -> p o", o=1), subg_t, writes=[subg_t])
    P.op("dve", MS(lams[:, :], 0.0), writes=[lams_t])
    for j in range(2):
        P.op("dve", STT(lamp[:, j, :], lamt[:, 2 * j, :], 1.0, lamt[:, 2 * j + 1, :], ALU.mult, ALU.mult, accum=lams[:, j:j + 1]),
             reads=[lam_t, lams_t], writes=[lamp_t, lams_t])
    P.op("act", ACTF(lams[:, 2:4], lams[:, 0:2], AF.Exp), reads=[lams_t], writes=[lams_t])
    P.op("dve", STT(neglam[:, :], lams[:, 3:4], -LAMBDA_INIT, lams[:, 2:3], ALU.add, ALU.subtract), reads=[lams_t], writes=[nl_t])
    P.op("dve", TS(subg[:, :], subg[:, :], (1.0 - LAMBDA_INIT), ALU.mult), reads=[subg_t], writes=[subg_t])
    P.barrier()
    arena.off = b_mark

    ubuf = []
    for i in range(2):
        ubuf.append(dict(KT=arena.tile([128, LP], BF16, "KT%d" % i), VA=arena.tile([128, NBLK, 130], BF16, "VA%d" % i),
                         QT=arena.tile([128, 2, 4096], BF16, "QT%d" % i)))
    for ub in ubuf:
        qv_, q_t_ = ub["QT"]
        P.op("dve", MS(qv_[64:128, 0, :], 0.0), writes=[q_t_])
        P.op("dve", MS(qv_[0:64, 1, :], 0.0), writes=[q_t_])
    Pt = [arena.tile([128, 512], BF16, "Pt%d" % i) for i in range(4)]
    onesb, onesb_t = arena.tile([128, 2, 128], BF16, "onesb")
    P.op("dve", MS(onesb, 1.0), writes=[onesb_t])
    P.op("dve", MS(onesb[0:NPAD, 1, :], 0.0), writes=[onesb_t])
    o1n, o1n_t = arena.tile([128, 512], F32, "o1n")
    uu, uu_t = arena.tile([128, 512], F32, "uu")
    usq, usq_t = arena.tile([128, 512], F32, "usq")
    rdb, rdb_t = arena.tile([128, 512], F32, "rdb")
    lnb, lnb_t = arena.tile([128, 512], F32, "lnb")
    rsb, rsb_t = arena.tile([128, 512], F32, "rsb")
    oTs = [arena.tile([128, 512], BF16, "oTs%d" % i) for i in range(2)]
    recf, recf_t = arena.tile([128, 512], F32, "recf")
    bcs, bcs_t = arena.tile([64, 512], F32, "bcs")
    accT = [T("acc0"), T("acc1")]
    psBf = psB[:, :].bitcast(F32)

    units = [("d", m) for m in range(4)] + [("f", h) for h in range(8)]

    def unit_load(ui):
        kind, idx = units[ui]
        ub = ubuf[ui % 2]
        (ktv, kt_t), (vav, va_t), (qtv, qt_t) = ub["KT"], ub["VA"], ub["QT"]
        if kind == "d":
            P.dma("sp", ktv[:, :], KTd_scr[idx], kt_t, writes=[kt_t])
            P.dma("sp", qtv[0:64, 0, :], QTd_scr[idx][0:64, :], qt_t, writes=[qt_t])
            P.dma("sp", qtv[64:128, 1, :], QTd_scr[idx][64:128, :], qt_t, writes=[qt_t])
            src = Vd_scr[:, idx, :].rearrange("(b k) n -> k b n", k=128)
            P.dma("sp", vav[:, :, :], src, va_t, writes=[va_t])
        else:
            P.dma("sp", ktv[0:65, :], KTf_scr[idx], kt_t, writes=[kt_t])
            P.dma("sp", qtv[0:65, 0, :], QTf_scr[idx], qt_t, writes=[qt_t])
            src = Vf_scr[:, idx, :].rearrange("(b k) n -> k b n", k=128)
            P.dma("sp", vav[:, :, 0:66], src, va_t, writes=[va_t])

    cnt = {"t": 0, "acc": 0, "ev": 0}

    def run_unit(ui):
        kind, idx = units[ui]
        ub = ubuf[ui % 2]
        (ktv, kt_t), (vav, va_t), (qtv, qt_t) = ub["KT"], ub["VA"], ub["QT"]
        dv = 128 if kind == "d" else 64
        maps = [0, 1] if kind == "d" else [0]
        tiles = []
        for i in range(NSLOT):
            for mp in maps:
                nk = 8 * i + 9
                for kb in range(nk):
                    diag = (8 * i + 1 <= kb <= 8 * i + 4)
                    other = kb > 8 * i + 4
                    rel = kb - (8 * i + 1) if diag else 0
                    tiles.append(dict(i=i, mp=mp, kb=kb, rel=rel, diag=diag, other=other, first=(kb == 0), last=(kb == nk - 1)))
        ntl = len(tiles)
        state = {}

        def emit_qk(tl):
            k = cnt["t"]
            cnt["t"] += 1
            bank = k % 3
            tl["pb"] = k % 4
            i, mp, kb = tl["i"], tl["mp"], tl["kb"]
            c0 = tl["rel"] * 128
            rows = slice(0, 128) if kind == "d" else slice(0, 65)
            P.op("pe", MM(psF[:, bank, c0:512], ktv[rows, kb * 128:(kb + 1) * 128], qtv[rows, mp, i * 512 + c0:(i + 1) * 512], True, True),
                 reads=[kt_t, qt_t], writes=[bankT[bank]])
            pv, p_t = Pt[tl["pb"]]
            if kind == "f":
                bias = biasS[i][0][:, kb, idx:idx + 1]
                rd = [bankT[bank], biasS[i][1]]
            elif tl["other"]:
                bias = flg[:, 0:1]
                rd = [bankT[bank], flg_t]
            else:
                bias = 0.0
                rd = [bankT[bank]]
            P.op("act", ACTF(pv[:, c0:512], psF[:, bank, c0:512], AF.Exp, bias=bias, scale=1.0), reads=rd, writes=[p_t])
            if tl["diag"]:
                sub = pv[:, c0:c0 + 128]
                P.op("dve", STT(sub, sub, 3.0e38, tri, ALU.min, ALU.mult), reads=[p_t, cb_t], writes=[p_t])

        deferred = []

        def zero_acc(aset):
            if kind == "d":
                P.op("dve", MS(psF[:, 3 + 2 * aset, :], 0.0), writes=[accT[aset]])
                P.op("dve", MS(psF[:, 4 + 2 * aset, :], 0.0), writes=[accT[aset]])
            else:
                P.op("dve", MS(psF[0:65, 3 + aset, :], 0.0), writes=[accT[aset]])

        def emit_av(tl, step):
            i, mp, kb = tl["i"], tl["mp"], tl["kb"]
            if tl["first"]:
                if (i, mp) not in state:
                    state[(i, mp)] = cnt["acc"] % 2
                    cnt["acc"] += 1
                    zero_acc(state[(i, mp)])
            aset = state[(i, mp)]
            pv, p_t = Pt[tl["pb"]]
            r0 = tl["rel"] if tl["diag"] else 0
            c0 = r0 * 128
            if kind == "d":
                P.op("pe", MM(psF[:, 3 + 2 * aset, c0:512], vav[:, kb, 0:128], pv[:, c0:512], False, False, skip=True),
                     reads=[p_t, va_t, accT[aset]], writes=[accT[aset]])
                P.op("pe", MM(psF[:, 4 + 2 * aset, c0:512], onesb[:, 1 if kb == 0 else 0, :], pv[:, c0:512], False, False, skip=True),
                     reads=[p_t, onesb_t, accT[aset]], writes=[accT[aset]])
            else:
                P.op("pe", MM(psF[0:65, 3 + aset, c0:512], vav[:, kb, 0:65], pv[:, c0:512], False, False, skip=True),
                     reads=[p_t, va_t, accT[aset]], writes=[accT[aset]])
            if tl["last"]:
                nxt = tl.get("next")
                if nxt is not None:
                    state[nxt] = cnt["acc"] % 2
                    cnt["acc"] += 1
                    zero_acc(state[nxt])
                if kind == "d":
                    emit_evac(i, mp, aset, step)
                else:
                    emit_evac_fox(i, aset, step)

        def emit_evac_fox(i, aset, step):
            k = cnt["ev"]
            cnt["ev"] += 1
            ov_, o_t = oTs[k % 2]
            P.op("dve", RCP(recf[64:65, :], psF[64:65, 3 + aset, :]), reads=[accT[aset]], writes=[recf_t])

            def stage_b():
                P.op("pe", MM(psF[0:64, 5, :], onesF[64:65, 0:64], recf[64:65, :], True, True), reads=[recf_t, cf_t], writes=[bankT[5]])

            def stage_c():
                P.op("dve", CP(bcs[:, :], psF[0:64, 5, :]), reads=[bankT[5]], writes=[bcs_t])
                P.op("dve", TT(ov_[0:64, :], psF[0:64, 3 + aset, :], bcs[:, :], ALU.mult), reads=[accT[aset], bcs_t], writes=[o_t])
                ch = 4 + idx // 2
                p0 = (idx % 2) * 64
                P.dma("pool", oT_scr[ch][p0:p0 + 64, i * 512:(i + 1) * 512], ov_[0:64, :], o_t, reads=[o_t])

            deferred.append((step + 6, stage_b))
            deferred.append((step + 8, stage_c))

        def emit_evac(i, mp, aset, step):
            k = cnt["ev"]
            cnt["ev"] += 1
            acc = psF[:, 3 + 2 * aset, :]
            den = psF[:, 4 + 2 * aset, :]
            P.op("dve", RCP(rdb[:, :], den), reads=[accT[aset]], writes=[rdb_t])
            if mp == 0:
                P.op("dve", TT(o1n[:, :], acc, rdb[:, :], ALU.mult), reads=[accT[aset], rdb_t], writes=[o1n_t])
                return
            P.op("dve", TT(uu[:, :], acc, rdb[:, :], ALU.mult), reads=[accT[aset], rdb_t], writes=[uu_t])
            P.op("dve", STT(uu[:, :], uu[:, :], neglam[:, 0:1], o1n[:, :], ALU.mult, ALU.add), reads=[uu_t, nl_t, o1n_t], writes=[uu_t])
            P.op("dve", TT(usq[:, :], uu[:, :], uu[:, :], ALU.mult), reads=[uu_t], writes=[usq_t])
            ov_, o_t = oTs[k % 2]

            def stage_b():
                P.op("pe", MM(psBf, onesF, usq[:, :], True, True), reads=[usq_t, cf_t], writes=[psBT])
                P.op("act", ACTF(lnb[:, :], psBf, AF.Ln, bias=1e-5, scale=1.0 / 128), reads=[psBT], writes=[lnb_t])
                P.op("act", ACTF(rsb[:, :], lnb[:, :], AF.Exp, scale=-0.5), reads=[lnb_t], writes=[rsb_t])

            def stage_c():
                P.op("dve", STT(ov_[:, :], uu[:, :], subg[:, 0:1], rsb[:, :], ALU.mult, ALU.mult), reads=[uu_t, subg_t, rsb_t], writes=[o_t])
                P.dma("pool", oT_scr[idx][:, i * 512:(i + 1) * 512], ov_[:, :], o_t, reads=[o_t])

            deferred.append((step + 9, stage_b))
            deferred.append((step + 13, stage_c))

        for t in range(ntl):
            if tiles[t]["last"] and t + 1 < ntl:
                tiles[t]["next"] = (tiles[t + 1]["i"], tiles[t + 1]["mp"])
        LAG = 2
        for t in range(ntl + LAG):
            if t < ntl:
                emit_qk(tiles[t])
            if t >= LAG:
                emit_av(tiles[t - LAG], t)
            deferred.sort(key=lambda d: d[0])
            while deferred and deferred[0][0] <= t:
                deferred.pop(0)[1]()
        while deferred:
            deferred.pop(0)[1]()

    unit_load(0)
    for ui in range(len(units)):
        if ui + 1 < len(units):
            unit_load(ui + 1)
        if ui == 4:
            P.barrier()
        run_unit(ui)
    end_phase()

    Wgu2 = arena.tile([128, 8, 2 * DFF], BF16, "Wgu2")
    Wd2 = arena.tile([128, NF, D], BF16, "Wd2")
    c_mark = arena.off
    Wo = arena.tile([128, 8, D], BF16, "Wo")
    stg = [arena.tile([128, 1024], F32, "stgc%d" % i) for i in range(3)]
    Wo_tk = load_weights(w_out, D, D, Wo, None, stg)
    Wgu2_tk, ch_a = weight_chunks(w_gu2, D, 2 * DFF, Wgu2, gc[:, 16:24], order=[0, 2, 3, 1, 4, 5])
    Wd2_tk, ch_b = weight_chunks(w_d2, DFF, D, Wd2, None)
    w2_chunks = ch_a + ch_b
    Wov, Wo_t = Wo
    oTb = [arena.tile([128, 8, 512], BF16, "oTb%d" % i) for i in range(2)]
    hb = [arena.tile([128, D], F32, "hb%d" % i) for i in range(6)]

    def c1_loads(i):
        ov_, o_t = oTb[i % 2]
        P.dma("sp", ov_[:, :, :], oT_scr[:, :, i * 512:(i + 1) * 512].rearrange("c p q -> p c q"), o_t, writes=[o_t])

    def c1_block(i, r, hk):
        ov_, o_t = oTb[i % 2]
        bv, b_t = hb[hk % 6]
        rows = slice(i * 512 + r * 128, i * 512 + (r + 1) * 128)
        P.dma("sp", bv[:, :], h1_scr[rows, :], b_t, writes=[b_t])
        for half in range(2):
            bank = next_bank()
            for c in range(8):
                P.op("pe", MM(psF[:, bank, :], ov_[:, c, r * 128:(r + 1) * 128], Wov[:, c, half * 512:(half + 1) * 512], c == 0, c == 7),
                     reads=[o_t, Wo_tk[0]], writes=[bankT[bank]])
            o = bv[:, half * 512:(half + 1) * 512]
            P.op("dve", TT(o, psF[:, bank, :], o, ALU.add), reads=[bankT[bank], b_t], writes=[b_t])
        P.dma("pool", h1_scr[rows, :], bv[:, :], b_t, reads=[b_t])

    c1_loads(0)
    hk = 0
    for i in range(NSLOT):
        if i + 1 < NSLOT:
            c1_loads(i + 1)
        for r in range(4):
            c1_block(i, r, hk)
            hk += 1
            for _ in range(3):
                if w2_chunks:
                    w2_chunks.pop(0)(stg)
    while w2_chunks:
        w2_chunks.pop(0)(stg)
    end_phase(mark=c_mark)

    own_groups = [(i * 512, 4, i) for i in range(NSLOT)]

    def c2_extra():
        ex = dict(gf=arena.tile([128, D], F32, "gf"), ss3=arena.tile([128, 4], F32, "ss3"),
                  ln3=arena.tile([128, 4], F32, "ln3"), rs3=arena.tile([128, 4], F32, "rs3"), k=[0])
        gv, g_t = ex["gf"]
        P.dma("sp", gv[:, :], gfin.partition_broadcast(128), g_t, writes=[g_t])
        return ex

    def c2_load(g, r, buf):
        tok0, nb, own = g
        bv, b_t = buf
        P.dma("sp", bv[:, :], h1_scr[tok0 + r * 128: tok0 + (r + 1) * 128, :], b_t, writes=[b_t])

    def c2_post(g, r, buf, ex):
        tok0, nb, own = g
        bv, b_t = buf
        k = ex["k"][0]
        ex["k"][0] += 1
        col = k % 4
        ssv, ss_t = ex["ss3"]
        jv, j_t = ex["junk"]
        lv, l_t = ex["ln3"]
        rv, r_t = ex["rs3"]
        gv, g_t = ex["gf"]
        sumsq(bv[:, :], b_t, jv[:, :], j_t, ssv[:, col:col + 1], ss_t)
        rstd_chain(ssv[:, col:col + 1], ss_t, lv[:, col:col + 1], l_t, rv[:, col:col + 1], r_t, 1.0 / D, 1e-6)
        P.op("dve", STT(bv[:, :], bv[:, :], rv[:, col:col + 1], gv[:, :], ALU.mult, ALU.mult), reads=[b_t, r_t, g_t], writes=[b_t])
        P.dma("pool", out_d[tok0 + r * 128: tok0 + (r + 1) * 128, :], bv[:, :], b_t, reads=[b_t])
        return None

    ffn_phase(w_gu2, w_d2, gc[:, 16:24], own_groups, c2_load, c2_post, c2_extra, preloaded=(Wgu2, Wd2, Wgu2_tk, Wd2_tk))

    with nc.Block() as block:
        @block.tensor
        def _(e):
            for f in P.streams["pe"]:
                f(e)

        @block.scalar
        def _(e):
            for f in P.streams["act"]:
                f(e)

        @block.vector
        def _(e):
            for f in P.streams["dve"]:
                f(e)

        @block.gpsimd
        def _(e):
            for f in P.streams["pool"]:
                f(e)

        @block.sync
        def _(e):
            for f in P.streams["sp"]:
                f(e)
    return nc, P


def _host_constants():
    ident = np.eye(128, dtype=np.float32)
    tri = np.triu(np.ones((128, 128), dtype=np.float32))
    cbf = np.concatenate([ident, tri], axis=1).astype(ml_dtypes.bfloat16)
    cf32 = np.concatenate([tri, np.ones((128, 128), np.float32), ident], axis=1).astype(np.float32)
    return cbf, cf32


def _rope_tables(pos, qscale):
    inv_freq = np.power(np.float32(500000.0), -np.arange(0, 16, 2, dtype=np.float32) / np.float32(16)).astype(np.float32)
    ang = (pos.astype(np.float32)[None, :] * inv_freq[:, None]).astype(np.float32)
    c = np.cos(ang).astype(np.float32)
    s = np.sin(ang).astype(np.float32)
    n = pos.shape[0]
    cosT = np.ones((64, n), np.float32)
    sinT = np.zeros((64, n), np.float32)
    cosT[0:8] = c
    cosT[8:16] = c
    sinT[0:8] = -s
    sinT[8:16] = s
    cosT = np.tile(cosT, (2, 1)) * np.float32(qscale)
    sinT = np.tile(sinT, (2, 1)) * np.float32(qscale)
    return np.ascontiguousarray(cosT), np.ascontiguousarray(sinT)


_CACHE = {}


def kernel(x, meta_tokens, ffn1_norm_g, ffn1_w_gate_up, ffn1_w_down, mix_norm_g, w_in, b_forget,
           lam_q1, lam_k1, lam_q2, lam_k2, diff_subln_g, w_out, ffn2_norm_g, ffn2_w_gate_up,
           ffn2_w_down, final_norm_g):
    f32 = np.float32
    x = np.asarray(x, f32)
    B = x.shape[0]
    if "nc" not in _CACHE:
        _CACHE["nc"] = build_program()[0]
    nc = _CACHE["nc"]
    cbf, cf32 = _host_constants()
    w_in0 = np.ascontiguousarray(np.asarray(w_in, f32)[0])
    perm = np.arange(1024)
    for base in range(0, 1024, 64):
        perm[base:base + 8] = np.arange(base + 8, base + 16)
        perm[base + 8:base + 16] = np.arange(base, base + 8)
    w_sw = np.ascontiguousarray(w_in0[:, perm])
    gcols = np.concatenate([np.asarray(g, f32)[0].reshape(8, 128).T for g in (ffn1_norm_g, mix_norm_g, ffn2_norm_g)], axis=1)
    gcols = np.ascontiguousarray(gcols)
    lam = np.ascontiguousarray(np.stack([np.asarray(v, f32)[0] for v in (lam_q1, lam_k1, lam_q2, lam_k2)]))
    shared = dict(
        w_gu1=np.ascontiguousarray(np.asarray(ffn1_w_gate_up, f32)[0]), w_d1=np.ascontiguousarray(np.asarray(ffn1_w_down, f32)[0]),
        w_gu2=np.ascontiguousarray(np.asarray(ffn2_w_gate_up, f32)[0]), w_d2=np.ascontiguousarray(np.asarray(ffn2_w_down, f32)[0]),
        w_in=w_in0, w_sw=w_sw, w_out=np.ascontiguousarray(np.asarray(w_out, f32)[0]), gcols=gcols,
        gfin=np.ascontiguousarray(np.asarray(final_norm_g, f32)), bfg=np.ascontiguousarray(np.asarray(b_forget, f32)[0]),
        lam=lam, subg=np.ascontiguousarray(np.asarray(diff_subln_g, f32)[0]), cbf=cbf, cf32=cf32)
    in_maps = []
    meta = np.asarray(meta_tokens, f32)
    for core in range(8):
        b, c = core // 2, core % 2
        xa = np.zeros((LP, D), f32)
        xa[NPAD:128] = meta
        pos = np.zeros(LP, f32)
        pos[0:128] = np.arange(128) - NPAD
        posq = np.zeros(4096, f32)
        for j in range(16):
            i = j // 2
            a = 2 * i + c if j % 2 == 0 else 2 * i + 1 - c
            xa[128 + 512 * j: 128 + 512 * (j + 1)] = x[b, 512 * a: 512 * (a + 1)]
            p = 128 + 512 * a + np.arange(512) - NPAD
            pos[128 + 512 * j: 128 + 512 * (j + 1)] = p
            if j % 2 == 0:
                posq[512 * i: 512 * (i + 1)] = p
        cK, sK = _rope_tables(pos, 1.0)
        cQ, sQ = _rope_tables(posq, 0.125)
        fl = np.zeros((128, 4), f32)
        fl[:, 0] = 0.0 if c == 1 else -30000.0
        fl[:, 1] = float(c)
        fl[:, 2] = float(1 - c)
        m = dict(shared)
        m.update(x_arr=xa, cosK=cK, sinK=sK, cosQ=cQ, sinQ=sQ, flags=fl)
        in_maps.append(m)
    res = run_bass_kernel_spmd(nc, in_maps, core_ids=list(range(8)))
    out = np.zeros((B, SEQ, D), f32)
    for core in range(8):
        b, c = core // 2, core % 2
        o = np.asarray(res.results[core]["out"])
        for i in range(NSLOT):
            a = 2 * i + c
            out[b, 512 * a: 512 * (a + 1)] = o[512 * i: 512 * (i + 1)]
    return out
```
